# Optimizing a Trainium2 kernel written in Bass

```python
import jax, jax.numpy as jnp
from jax import lax
import numpy as np

D_MODEL = 1024
BATCH = 8
SEQ = 2048
DEPTH = 4
DEC_BATCH = 128
DEC_SEQ = 8
PAST_LEN = 16384
PAGE_SIZE = 128

D_CONV = D_MODEL // 2
CONV_W = 3
D_GMLP = D_MODEL // 2
G_HEADS = 8
G_HEAD_DIM = D_GMLP // G_HEADS
CHUNK = 128
D_POOL = D_MODEL // 2
POOL_WINDOWS = (2, 4, 8, 16)
POOL_GROUPS = len(POOL_WINDOWS)
POOL_GDIM = D_POOL // POOL_GROUPS
POOL_OUT_GDIM = D_MODEL // POOL_GROUPS
MAX_WIN = max(POOL_WINDOWS)
D_FF = 2816
N_SUB = 3
EPS = 1e-6
IN_SIZES = (D_CONV, D_CONV, D_CONV, D_GMLP, D_GMLP, D_POOL, D_MODEL, D_MODEL, D_MODEL)
IN_COLS = sum(IN_SIZES)
IN_SPLITS = tuple(int(s) for s in np.cumsum(IN_SIZES)[:-1])

kernel_name = 'hybrid_conv_gmlp_pool_decoder_step'


def _rmsnorm(x, g):
    xf = x.astype(jnp.float32)
    y = xf * lax.rsqrt(jnp.mean(xf * xf, axis=-1, keepdims=True) + EPS)
    return (y * g.astype(jnp.float32)).astype(x.dtype)


def _layernorm(x, g):
    xf = x.astype(jnp.float32)
    mu = jnp.mean(xf, axis=-1, keepdims=True)
    var = jnp.mean(jnp.square(xf - mu), axis=-1, keepdims=True)
    return ((xf - mu) * lax.rsqrt(var + EPS) * g.astype(jnp.float32)).astype(x.dtype)


def _modulate(h, shift, scale):
    return h * (1 + scale[:, None, :]) + shift[:, None, :]


def _swiglu(h, w_gu, w_dn):
    a, b = jnp.split(h @ w_gu, 2, axis=-1)
    return (jax.nn.silu(a) * b) @ w_dn


def _short_conv(ext, conv_w, L):
    return sum(ext[:, k:k + L] * conv_w[k] for k in range(CONV_W))


def _spatial_gate(v, w_s, b_s):
    bsz, L, _ = v.shape
    w = w_s * jnp.tril(jnp.ones((CHUNK, CHUNK), w_s.dtype))
    bias = b_s.T
    if L < CHUNK:
        vh = v.reshape(bsz, L, G_HEADS, G_HEAD_DIM)
        out = jnp.einsum('hts,bshd->bthd', w[:, :L, :L], vh) + bias[None, :L, :, None]
        return out.reshape(bsz, L, D_GMLP)
    n = -(-L // CHUNK)
    vp = jnp.pad(v, ((0, 0), (0, n * CHUNK - L), (0, 0))).reshape(bsz, n, CHUNK, G_HEADS, G_HEAD_DIM)
    out = jnp.einsum('hts,bnshd->bnthd', w, vp) + bias[None, None, :, :, None]
    return out.reshape(bsz, n * CHUNK, D_GMLP)[:, :L]


def _multi_pool(ext, L, pos0):
    hist = ext.shape[1] - L
    ef = ext.astype(jnp.float32)
    cs = jnp.concatenate([jnp.zeros_like(ef[:, :1]), jnp.cumsum(ef, axis=1)], axis=1)
    pos = pos0 + jnp.arange(L, dtype=jnp.int32)
    outs = []
    for gi, w in enumerate(POOL_WINDOWS):
        cg = cs[:, :, gi * POOL_GDIM:(gi + 1) * POOL_GDIM]
        s = cg[:, hist + 1:hist + 1 + L] - cg[:, hist + 1 - w:hist + 1 - w + L]
        cnt = jnp.minimum(pos + 1, w).astype(jnp.float32)
        outs.append(s / cnt[None, :, None])
    return jnp.concatenate(outs, axis=-1).astype(ext.dtype)


def _mixer(h, hist_conv, hist_pool, pos0, w_in, conv_w, w_out_a, ln_g, w_s, b_s, w_out_b,
           pool_w, pool_scale, w_o):
    bsz, L, _ = h.shape
    xa, bg, cg, u, v, p, ga, gb, gc = jnp.split(h @ w_in, IN_SPLITS, axis=-1)
    ext_c = jnp.concatenate([hist_conv, cg * xa], axis=1)
    y_a = (bg * _short_conv(ext_c, conv_w, L)) @ w_out_a
    new_conv = ext_c[:, -(CONV_W - 1):]
    u = jax.nn.gelu(u, approximate=False)
    v = _layernorm(jax.nn.gelu(v, approximate=False), ln_g)
    y_b = (u * _spatial_gate(v, w_s, b_s)) @ w_out_b
    v_open = v[:, (L // CHUNK) * CHUNK:]
    ext_p = jnp.concatenate([hist_pool, p], axis=1)
    pooled = (_multi_pool(ext_p, L, pos0) - p).reshape(bsz, L, POOL_GROUPS, POOL_GDIM)
    y_c = jnp.einsum('blgc,gcd->blgd', pooled, pool_w).reshape(bsz, L, D_MODEL) * pool_scale
    new_pool = ext_p[:, -(MAX_WIN - 1):]
    merged = jax.nn.sigmoid(ga) * y_a + jax.nn.sigmoid(gb) * y_b + jax.nn.sigmoid(gc) * y_c
    return merged @ w_o, new_conv, new_pool, v_open


def _layer(x, c, hist_conv, hist_pool, pos0, norm_g, w_ada, b_ada, w1_gu, w1_dn, w2_gu, w2_dn,
           w_in, conv_w, w_out_a, ln_g, w_s, b_s, w_out_b, pool_w, pool_scale, w_o):
    bsz = x.shape[0]
    mod = (jax.nn.silu(c) @ w_ada + b_ada).reshape(bsz, N_SUB, 3, D_MODEL)
    h = _modulate(_rmsnorm(x, norm_g[0]), mod[:, 0, 0], mod[:, 0, 1])
    x = x + 0.5 * mod[:, 0, 2][:, None, :] * _swiglu(h, w1_gu, w1_dn)
    h = _modulate(_rmsnorm(x, norm_g[1]), mod[:, 1, 0], mod[:, 1, 1])
    m, new_conv, new_pool, v_open = _mixer(h, hist_conv, hist_pool, pos0, w_in, conv_w, w_out_a,
                                           ln_g, w_s, b_s, w_out_b, pool_w, pool_scale, w_o)
    x = x + mod[:, 1, 2][:, None, :] * m
    h = _modulate(_rmsnorm(x, norm_g[2]), mod[:, 2, 0], mod[:, 2, 1])
    x = x + 0.5 * mod[:, 2, 2][:, None, :] * _swiglu(h, w2_gu, w2_dn)
    return x, new_conv, new_pool, v_open


def setup_inputs(seed: int = 0) -> dict:
    key = jax.random.key(seed)
    ks = jax.random.split(key, 32)
    f32 = jnp.float32

    def nrm(k, shape, scale):
        return jax.random.normal(k, shape, f32) * scale

    return {
        'x_prompt': nrm(ks[0], (BATCH, SEQ, D_MODEL), 1.0),
        'x_sample': nrm(ks[1], (DEC_BATCH, DEC_SEQ, D_MODEL), 1.0),
        'state_conv': nrm(ks[2], (DEPTH, DEC_BATCH, CONV_W - 1, D_CONV), 1.0),
        'state_pool': nrm(ks[3], (DEPTH, DEC_BATCH, MAX_WIN - 1, D_POOL), 1.0),
        'c_prompt': nrm(ks[4], (BATCH, D_MODEL), 1.0),
        'c_sample': nrm(ks[5], (DEC_BATCH, D_MODEL), 1.0),
        'norm_g': 1.0 + nrm(ks[6], (DEPTH, N_SUB, D_MODEL), 0.05),
        'w_ada': nrm(ks[7], (DEPTH, D_MODEL, N_SUB * 3 * D_MODEL), 0.5 * D_MODEL ** -0.5),
        'b_ada': nrm(ks[8], (DEPTH, N_SUB * 3 * D_MODEL), 0.02),
        'w1_gu': nrm(ks[9], (DEPTH, D_MODEL, 2 * D_FF), D_MODEL ** -0.5),
        'w1_dn': nrm(ks[10], (DEPTH, D_FF, D_MODEL), D_FF ** -0.5),
        'w2_gu': nrm(ks[11], (DEPTH, D_MODEL, 2 * D_FF), D_MODEL ** -0.5),
        'w2_dn': nrm(ks[12], (DEPTH, D_FF, D_MODEL), D_FF ** -0.5),
        'w_in': nrm(ks[13], (DEPTH, D_MODEL, IN_COLS), D_MODEL ** -0.5),
        'conv_w': nrm(ks[14], (DEPTH, CONV_W, D_CONV), CONV_W ** -0.5),
        'w_out_a': nrm(ks[15], (DEPTH, D_CONV, D_MODEL), D_CONV ** -0.5),
        'ln_g': 1.0 + nrm(ks[16], (DEPTH, D_GMLP), 0.05),
        'w_s': nrm(ks[17], (DEPTH, G_HEADS, CHUNK, CHUNK), CHUNK ** -0.5),
        'b_s': 1.0 + nrm(ks[18], (DEPTH, G_HEADS, CHUNK), 0.1),
        'w_out_b': nrm(ks[19], (DEPTH, D_GMLP, D_MODEL), D_GMLP ** -0.5),
        'pool_w': nrm(ks[20], (DEPTH, POOL_GROUPS, POOL_GDIM, POOL_OUT_GDIM), POOL_GDIM ** -0.5),
        'pool_scale': 1.0 + nrm(ks[21], (DEPTH, D_MODEL), 0.1),
        'w_o': nrm(ks[22], (DEPTH, D_MODEL, D_MODEL), D_MODEL ** -0.5),
        'final_norm_g': 1.0 + nrm(ks[23], (D_MODEL,), 0.05),
    }


def reference(x_prompt, x_sample, state_conv, state_pool, c_prompt, c_sample, norm_g, w_ada, b_ada,
              w1_gu, w1_dn, w2_gu, w2_dn, w_in, conv_w, w_out_a, ln_g, w_s, b_s, w_out_b,
              pool_w, pool_scale, w_o, final_norm_g):
    xp, xs = x_prompt, x_sample
    bp = xp.shape[0]
    zero_conv = jnp.zeros((bp, CONV_W - 1, D_CONV), xp.dtype)
    zero_pool = jnp.zeros((bp, MAX_WIN - 1, D_POOL), xp.dtype)
    conv_p, conv_s, pool_p, pool_s, v_s = [], [], [], [], []
    for l in range(DEPTH):
        prm = (norm_g[l], w_ada[l], b_ada[l], w1_gu[l], w1_dn[l], w2_gu[l], w2_dn[l], w_in[l],
               conv_w[l], w_out_a[l], ln_g[l], w_s[l], b_s[l], w_out_b[l], pool_w[l],
               pool_scale[l], w_o[l])
        xp, ncp, npp, _ = _layer(xp, c_prompt, zero_conv, zero_pool, 0, *prm)
        xs, ncs, nps, vs = _layer(xs, c_sample, state_conv[l], state_pool[l], PAST_LEN, *prm)
        conv_p.append(ncp)
        pool_p.append(npp)
        conv_s.append(ncs)
        pool_s.append(nps)
        v_s.append(vs)
    y_prompt = _rmsnorm(xp, final_norm_g)
    y_sample = _rmsnorm(xs, final_norm_g)
    new_conv_prompt = jnp.stack(conv_p)
    new_conv_sample = jnp.stack(conv_s)
    new_pool_prompt = jnp.stack(pool_p)
    new_pool_sample = jnp.stack(pool_s)
    new_gmlp_v_sample = jnp.stack(v_s)
    return (y_prompt, y_sample, new_conv_prompt, new_conv_sample, new_pool_prompt, new_pool_sample, new_gmlp_v_sample)
```

```python
import numpy as np
from contextlib import ExitStack
import concourse.bass as bass
import concourse.mybir as mybir
from concourse.bass_utils import run_bass_kernel_spmd

F32 = mybir.dt.float32
BF16 = mybir.dt.bfloat16
AF = mybir.ActivationFunctionType
ALU = mybir.AluOpType
ENGS = ['pe', 'act', 'dve', 'pool', 'sp']
CENGS = ['pe', 'act', 'dve', 'pool']

D = 1024
NL = 4
T = 2176
TP = 2048
DFF = 2816
NFF = 22
EPS = 1e-6
TT_ALL = [(0, 512), (512, 512), (1024, 512), (1536, 512), (2048, 128)]
PARTS = [[(0, 512), (512, 256)], [(768, 512), (1280, 256)], [(1536, 512), (2048, 128)]]
FF_GROUPS = [list(range(0, 8)), list(range(8, 16)), list(range(16, 22))]
NSLOT = 4
WINS = (2, 4, 8, 16)


class Prog:
    def __init__(self):
        self.q = {e: [] for e in ENGS}
        self.cnt = {}
        self.seen = {e: {} for e in ENGS}
        self.floor = []

    def _waits(self, eng, deps, use_floor=True):
        waits = []
        for d in list(deps) + (self.floor if use_floor else []):
            if d is None:
                continue
            name, v = d
            if name == eng and False:
                continue
            if v > self.seen[eng].get(name, 0):
                self.seen[eng][name] = v
                waits.append((name, v))
        return waits

    def op(self, eng, fn, deps=()):
        waits = self._waits(eng, deps)
        self.cnt[eng] = self.cnt.get(eng, 0) + 1
        self.q[eng].append((waits, fn, (eng, 1)))
        return (eng, self.cnt[eng])

    def dma(self, eng, fn, sem, deps=()):
        waits = self._waits(eng, deps, use_floor=False)
        self.cnt[sem] = self.cnt.get(sem, 0) + 16
        self.q[eng].append((waits, fn, (sem, 16)))
        return (sem, self.cnt[sem])

    def wait(self, eng, deps):
        waits = self._waits(eng, deps)
        if waits:
            self.q[eng].append((waits, None, None))

    def now(self):
        return [(e, self.cnt[e]) for e in CENGS if self.cnt.get(e, 0) > 0]

    def barrier(self):
        self.floor = self.now()

    def emit(self, nc, sems):
        with nc.Block() as block:
            def run(eng_name):
                def f(h):
                    for waits, fn, inc in self.q[eng_name]:
                        for (name, v) in waits:
                            h.wait_ge(sems[name], v)
                        if fn is not None:
                            ins = fn(h)
                            ins.then_inc(sems[inc[0]], inc[1])
                return f
            block.tensor(run('pe'))
            block.scalar(run('act'))
            block.vector(run('dve'))
            block.gpsimd(run('pool'))
            block.sync(run('sp'))


class Ring:
    def __init__(self, bufs):
        self.bufs = bufs
        self.rel = [[] for _ in bufs]
        self.i = 0

    def acquire(self):
        i = self.i
        self.i = (self.i + 1) % len(self.bufs)
        deps = self.rel[i]
        self.rel[i] = []
        return i, self.bufs[i], deps

    def release(self, i, toks):
        self.rel[i] = list(toks)


def build(nl=NL):
    nc = bass.Bass("TRN2", target_bir_lowering=False)
    P = Prog()

    def din(name, shape):
        return nc.dram_tensor(name, list(shape), F32, kind="ExternalInput").ap()

    def dout(name, shape):
        return nc.dram_tensor(name, list(shape), F32, kind="ExternalOutput").ap()

    xp_d = din("xp", [TP, D]); xs_d = din("xs", [128, D]); cT_d = din("cT", [128, 8, 17])
    sconvT_d = din("sconvT", [NL, 128, 4, 16, 10]); spoolT_d = din("spoolT", [NL, 128, 4, 16, 23])
    spool_raw_d = din("spool_raw", [NL, 16, 15, 512])
    vecs_d = din("vecs", [NL, 128, 116]); fin_d = din("fin", [128, 8]); lng_d = din("lng", [NL, 512])
    biasp_d = din("biasp", [NL, 128, 4, 128]); biass_d = din("biass", [NL, 128, 4, 128])
    wsT_d = din("wsT", [NL, 128, 8, 128]); ws8_d = din("ws8", [NL, 128, 8, 128])
    masks_d = din("masks", [128, 2, 128]); invcnt_d = din("invcnt", [128, 4, 16]); ident_d = din("ident", [128, 128])
    w_ada_d = din("w_ada", [NL, D, 9216]); w1gu_d = din("w1_gu", [NL, D, 2 * DFF]); w1dn_d = din("w1_dn", [NL, DFF, D])
    w2gu_d = din("w2_gu", [NL, D, 2 * DFF]); w2dn_d = din("w2_dn", [NL, DFF, D]); w_in_d = din("w_in", [NL, D, 6144])
    woa_d = din("w_out_a", [NL, 512, D]); wob_d = din("w_out_b", [NL, 512, D]); poolw_d = din("pool_w", [NL, 4, 128, 256])
    wo_d = din("w_o", [NL, D, D])
    yp_d = dout("yp", [TP, D]); ys_d = dout("ys", [128, D])
    ncp_d = dout("ncp", [NL, 2, 512]); ncs_d = dout("ncs", [NL, 16, 2, 512])
    npp_d = dout("npp", [NL, 15, 512]); nps_d = dout("nps", [NL, 16, 15, 512]); nvs_d = dout("nvs", [NL, 128, 512])

    with ExitStack() as es:
        def sb(name, shape, dt):
            return es.enter_context(nc.sbuf_tensor('s_' + name, list(shape), dt))

        xT = sb("xT", [128, 8, T], F32)
        ARENA = 45568
        arena = sb("arena", [128, ARENA], BF16)
        ring_t = [sb(f"ring{i}", [128, 8, 256], BF16) for i in range(NSLOT)]
        ident = sb("ident", [128, 128], F32)
        ones = sb("ones", [128, 128], BF16)
        masks = sb("masks", [128, 2, 128], F32)
        invcnt = sb("invcnt", [128, 4, 16], F32)
        fin = sb("fin", [128, 8], F32)
        scT = sb("scT", [128, 8, 17], BF16)
        vecs = sb("vecs", [128, 116], F32)
        lng = sb("lngbc", [128, 512], F32)
        biasp = sb("biasp", [128, 4, 128], F32)
        biass = sb("biass", [128, 4, 128], F32)
        wsT = sb("wsT", [128, 8, 128], BF16)
        ws8 = sb("ws8", [128, 8, 128], BF16)
        poolw = sb("poolw", [128, 4, 256], BF16)
        modT = sb("modT", [128, 72, 17], F32)
        At = sb("At", [128, 3, 8, 17], F32)
        Gt = sb("Gt", [128, 3, 8, 17], F32)
        qhist = sb("qhist", [128, 4, 2], F32)
        phist = sb("phist", [128, 4, 15], F32)
        qs = sb("qs", [128, 4, 16, 10], F32)
        pss = sb("pss", [128, 4, 16, 23], F32)
        small = sb("small", [128, 64], F32)
        epsb = sb("epsb", [128, 1], F32)
        mstg = sb("mstg", [32, 512], F32)
        psb = [es.enter_context(nc.psum_tensor(f"psb{i}", [128, 512], F32)) for i in range(8)]

        sem_names = ENGS + [f"ring{i}" for i in range(NSLOT)] + ["dn", "cst", "xin", "lay", "hist", "out", "stg0", "stg1", "nvs", "bt", "layp"]
        sems = {n: es.enter_context(nc.semaphore(n)) for n in sem_names}

        PS = Ring(psb[0:7])
        modB = psb[7]
        RG = Ring(ring_t)

        def av(off, shape, dt=BF16):
            n = 1
            for s in shape[1:]:
                n *= s
            if dt == F32:
                assert off % 2 == 0
                v = arena[:, off:off + 2 * n].bitcast(F32)
            else:
                v = arena[:, off:off + n]
            if len(shape) == 2:
                return v
            if len(shape) == 3:
                return v.rearrange("p (a b) -> p a b", a=shape[1])
            if len(shape) == 4:
                return v.rearrange("p (a b c) -> p a b c", a=shape[1], b=shape[2])
            raise ValueError

        H_OFF = 0
        G_OFF = 17408
        DN_OFF = 34816
        FT_OFF = 43008
        MH_OFF = 0
        ZA_OFF = 6144; ZB_OFF = 9216; ZC_OFF = 12288
        MG_OFF = 15360
        VN_OFF = 21504
        TMP_OFF = 24576
        hfull = av(H_OFF, [128, 8, T])
        gbuf = av(G_OFF, [128, 8, T])
        dnbuf = av(DN_OFF, [128, 8, 1024])
        silu_t = [av(FT_OFF + 1024 * i, [128, 512], F32) for i in range(2)]
        gate_t = av(FT_OFF + 2048, [128, 128], F32)
        hpart = av(MH_OFF, [128, 8, 768])
        zA = av(ZA_OFF, [128, 4, 768]); zB = av(ZB_OFF, [128, 4, 768]); zC = av(ZC_OFF, [128, 4, 768])
        merged = av(MG_OFF, [128, 8, 768])
        vn = av(VN_OFF, [128, 6, 512])

        cst = []
        cst.append(P.dma('sp', lambda h: h.dma_start(out=ident[:], in_=ident_d), 'cst'))
        cst.append(P.dma('sp', lambda h: h.dma_start(out=masks[:], in_=masks_d), 'cst'))
        cst.append(P.dma('sp', lambda h: h.dma_start(out=invcnt[:], in_=invcnt_d), 'cst'))
        cst.append(P.dma('sp', lambda h: h.dma_start(out=fin[:], in_=fin_d), 'cst'))
        cTf = av(G_OFF + 8192, [128, 8, 17], F32)
        cst.append(P.dma('sp', lambda h: h.dma_start(out=cTf, in_=cT_d), 'cst'))
        t_cst = cst[-1]
        t_ones = P.op('dve', lambda h: h.memset(ones[:], 1.0))
        t_eps = P.op('dve', lambda h: h.memset(epsb[:], EPS))
        t_sc = P.op('act', lambda h: h.activation(out=scT[:], in_=cTf, func=AF.Silu), [t_cst])

        def wslab(srcs):
            i, slot, deps = RG.acquire()
            tok = None
            for ent in srcs:
                if len(ent) == 3:
                    src, c0, n = ent
                    k0, k1 = 0, 8
                else:
                    src, c0, n, k0, k1 = ent
                tok = P.dma('pool', (lambda h, slot=slot, src=src, c0=c0, n=n, k0=k0, k1=k1: h.dma_start(out=slot[:, k0:k1, c0:c0 + n], in_=src)),
                            f"ring{i}", deps)
            return i, slot, tok

        def wcols(w2d, col, n=128):
            return w2d.rearrange("(k p) n -> p k n", p=128)[:, :, col:col + n]

        def evac_bank():
            return PS.acquire()

        stage = [av(G_OFF + 2048 * i, [128, 1024], F32) for i in range(2)]
        stage_rel = [[], []]
        xw_tok = {}
        for ti in range(17):
            b = ti % 2
            src = xp_d[ti * 128:(ti + 1) * 128, :] if ti < 16 else xs_d
            t_ld = P.dma('sp', (lambda h, b=b, src=src: h.dma_start(out=stage[b], in_=src)), f"stg{b}", stage_rel[b])
            toks = []
            last_tr = None
            for a in range(2):
                bi, bank, bdeps = PS.acquire()
                tr = None
                for j in range(4):
                    dc = a * 4 + j
                    tr = P.op('pe', (lambda h, bank=bank, b=b, dc=dc, j=j: h.transpose(
                        out=bank[:, j * 128:(j + 1) * 128], in_=stage[b][:, dc * 128:(dc + 1) * 128], identity=ident[:])),
                        [t_ld, t_cst] + bdeps)
                last_tr = tr
                eng = 'act' if a == 0 else 'dve'
                if eng == 'act':
                    tc = P.op('act', (lambda h, bank=bank, a=a, ti=ti: h.activation(
                        out=xT[:, a * 4:(a + 1) * 4, ti * 128:(ti + 1) * 128],
                        in_=bank[:].rearrange("p (a b) -> p a b", a=4), func=AF.Copy)), [tr])
                else:
                    tc = P.op('dve', (lambda h, bank=bank, a=a, ti=ti: h.tensor_copy(
                        out=xT[:, a * 4:(a + 1) * 4, ti * 128:(ti + 1) * 128],
                        in_=bank[:].rearrange("p (a b) -> p a b", a=4))), [tr])
                PS.release(bi, [tc])
                toks.append(tc)
            stage_rel[b] = [last_tr]
            xw_tok[ti] = toks

        def xdeps(c0, n):
            r = []
            for ti in range(c0 // 128, (c0 + n + 127) // 128):
                r += xw_tok.get(ti, [])
            return r

        def set_xw(c0, n, toks):
            for ti in range(c0 // 128, (c0 + n + 127) // 128):
                xw_tok[ti] = list(toks)

        lay_tok = {}

        def load_layer_small(l, d=None):
            if d is None:
                d = P.now()
            t = []
            t.append(P.dma('sp', lambda h: h.dma_start(out=vecs[:], in_=vecs_d[l]), 'lay', d))
            t.append(P.dma('sp', lambda h: h.dma_start(out=lng[:], in_=lng_d[l:l + 1, :].partition_broadcast(128)), 'lay', d))
            t.append(P.dma('sp', lambda h: h.dma_start(out=biasp[:], in_=biasp_d[l]), 'lay', d))
            t.append(P.dma('sp', lambda h: h.dma_start(out=biass[:], in_=biass_d[l]), 'lay', d))
            t_sp = t[-1]
            t1 = P.dma('pool', lambda h: h.dma_start(out=wsT[:], in_=wsT_d[l]), 'layp', d)
            t2 = P.dma('pool', lambda h: h.dma_start(out=ws8[:], in_=ws8_d[l]), 'layp', d)
            t3 = P.dma('pool', lambda h: h.dma_start(out=poolw[:], in_=poolw_d[l].rearrange("g c d -> c g d")), 'layp', d)
            m1 = P.op('pool', lambda h: h.tensor_tensor(out=wsT[:], in0=wsT[:], in1=masks[:, 0:1, :].broadcast_to([128, 8, 128]), op=ALU.mult), [t3, t_sp, t_cst])
            m2 = P.op('pool', lambda h: h.tensor_tensor(out=ws8[:], in0=ws8[:], in1=masks[:, 1:2, :].broadcast_to([128, 8, 128]), op=ALU.mult), [m1])
            lay_tok[l] = m2
            return m2

        class ModJob:
            def __init__(self, l):
                self.l = l
                self.ct = 0
                self.toks = []

            def step(self):
                self._flush()
                if self.ct >= 18:
                    return
                l = self.l
                ct = self.ct
                self.ct += 1
                bi, bank, bdeps = PS.acquire()
                last = None
                for hf in range(2):
                    si, slot, wt = wslab([(wcols(w_ada_d[l], ct * 512 + hf * 256, 256), 0, 256)])
                    for k in range(8):
                        last = P.op('pe', (lambda h, bank=bank, slot=slot, hf=hf, k=k: h.matmul(
                            bank[0:17, hf * 256:(hf + 1) * 256], lhsT=scT[:, k, :], rhs=slot[:, k, :],
                            start=(k == 0), stop=(k == 7))), [wt, t_sc] + bdeps)
                    RG.release(si, [last])
                tcp = P.op('act', (lambda h, bank=bank: h.activation(out=mstg[0:17, :], in_=bank[0:17, :], func=AF.Copy)), [last] + modst['stg_rel'])
                PS.release(bi, [tcp])
                self.pending = (ct, tcp)

            def _flush(self):
                if getattr(self, 'pending', None) is None:
                    return
                ct, tcp = self.pending
                self.pending = None
                l = self.l
                tr = None
                for j in range(4):
                    cc = (ct % 6) * 4 + j
                    tr = P.op('pe', (lambda h, j=j, cc=cc: h.transpose(
                        out=modB[:, cc * 17:(cc + 1) * 17], in_=mstg[0:17, j * 128:(j + 1) * 128], identity=ident[0:17, 0:17])),
                        [tcp, t_cst] + modst['B_rel'])
                modst['stg_rel'] = [tr]
                if ct % 6 == 5:
                    bnk = ct // 6
                    t = P.op('dve', (lambda h, bnk=bnk: h.tensor_tensor(
                        out=modT[:, bnk * 24:(bnk + 1) * 24, :], in0=modB[:, 0:408].rearrange("p (a b) -> p a b", a=24),
                        in1=vecs[:, bnk * 24:(bnk + 1) * 24].unsqueeze(2).broadcast_to([128, 24, 17]), op=ALU.add)),
                        [tr, lay_tok[l]] + getattr(self, 'extra', []))
                    modst['B_rel'] = [t]
                    self.toks.append(t)

            def finish(self):
                while self.ct < 18:
                    self.step()
                self._flush()
                return self.toks

        modst = {'stg_rel': [], 'B_rel': []}

        def finish_mod(l, toks, subs=(0, 1, 2)):
            out = []
            for sub in subs:
                ta = P.op('dve', (lambda h, sub=sub: h.scalar_tensor_tensor(
                    out=At[:, sub, :, :], in0=modT[:, sub * 24 + 8:sub * 24 + 16, :], scalar=1.0,
                    in1=vecs[:, 72 + sub * 8:72 + sub * 8 + 8].unsqueeze(2).broadcast_to([128, 8, 17]),
                    op0=ALU.add, op1=ALU.mult)), toks)
                tg = P.op('dve', (lambda h, sub=sub: h.tensor_scalar(
                    out=Gt[:, sub, :, :], in0=modT[:, sub * 24 + 16:sub * 24 + 24, :], scalar1=(1.0 if sub == 1 else 0.5), scalar2=None,
                    op0=ALU.mult)), toks)
                out += [ta, tg]
            return out

        def Bsh(sub, k, s0, s1):
            return modT[:, sub * 24 + k, s0:s1]

        def norm(sub, tts, hdst, hcol0, noff, mod_toks, final=False, extra=()):
            sqh = [av(noff + 2048 * i, [128, 4, 512]) for i in range(2)]
            tmpk = [av(noff + 4096 + 1024 * i, [128, 512], F32) for i in range(4)]
            tmps = av(noff + 4096, [128, 8, 128], F32)
            tmp2 = [av(noff + 4096 + 2048 * i, [128, 2, 512], F32) for i in range(2)]
            htoks = {}
            sq_rel = [[], []]
            held = {}
            for (c0, n) in tts:
                bi, bank, bdeps = PS.acquire()
                mm = None
                for hf in range(2):
                    t_sq = P.op('act', (lambda h, c0=c0, n=n, hf=hf: h.activation(out=sqh[hf][:, :, 0:n], in_=xT[:, hf * 4:hf * 4 + 4, c0:c0 + n], func=AF.Square)),
                                xdeps(c0, n) + sq_rel[hf] + list(extra))
                    for k in range(4):
                        mm = P.op('pe', (lambda h, bank=bank, k=k, n=n, hf=hf: h.matmul(bank[:, 0:n], lhsT=ones[:], rhs=sqh[hf][:, k, 0:n],
                                                                                        start=(hf == 0 and k == 0), stop=(hf == 1 and k == 3))), [t_sq, t_ones] + bdeps)
                    sq_rel[hf] = [mm]
                t_rt = P.op('act', (lambda h, bank=bank, n=n: h.activation(out=bank[:, 0:n], in_=bank[:, 0:n], func=AF.Sqrt,
                                                                         scale=1.0 / D, bias=epsb[:, 0:1])), [mm, t_eps])
                t_rs = P.op('dve', (lambda h, bank=bank, n=n: h.reciprocal(out=bank[:, 0:n], in_=bank[:, 0:n])), [t_rt])
                held[(c0, n)] = (bi, bank, t_rs)
            tk_rel = [[], [], [], []]
            kk = 0
            for (c0, n) in tts:
                bi, bank, t_rs = held[(c0, n)]
                toks = []
                if final:
                    for k in range(8):
                        t = P.op('dve', (lambda h, bank=bank, k=k, c0=c0, n=n: h.scalar_tensor_tensor(
                            out=xT[:, k, c0:c0 + n], in0=xT[:, k, c0:c0 + n], scalar=fin[:, k:k + 1], in1=bank[:, 0:n],
                            op0=ALU.mult, op1=ALU.mult)), [t_rs])
                        toks.append(t)
                    set_xw(c0, n, toks)
                elif c0 < TP:
                    for kp in range(4):
                        b = kk % 2
                        kk += 1
                        t1 = P.op('dve', (lambda h, bank=bank, kp=kp, c0=c0, n=n, b=b: h.tensor_tensor(
                            out=tmp2[b][:, :, 0:n], in0=xT[:, 2 * kp:2 * kp + 2, c0:c0 + n],
                            in1=bank[:, 0:n].unsqueeze(1).broadcast_to([128, 2, n]), op=ALU.mult)), [t_rs] + tk_rel[b])
                        pair = []
                        for j in range(2):
                            k = 2 * kp + j
                            if k in (2, 5):
                                t2 = P.op('dve', (lambda h, k=k, j=j, c0=c0, n=n, b=b: h.tensor_scalar(
                                    out=hdst[:, k, c0 - hcol0:c0 - hcol0 + n], in0=tmp2[b][:, j, 0:n], scalar1=At[:, sub, k, 0:1], scalar2=Bsh(sub, k, 0, 1),
                                    op0=ALU.mult, op1=ALU.add)), [t1] + mod_toks)
                            else:
                                t2 = P.op('act', (lambda h, k=k, j=j, c0=c0, n=n, b=b: h.activation(
                                    out=hdst[:, k, c0 - hcol0:c0 - hcol0 + n], in_=tmp2[b][:, j, 0:n], func=AF.Identity,
                                    scale=At[:, sub, k, 0:1], bias=Bsh(sub, k, 0, 1))), [t1] + mod_toks)
                            toks.append(t2)
                            pair.append(t2)
                        tk_rel[b] = pair
                    last_t1 = t1
                else:
                    t1 = P.op('dve', (lambda h, bank=bank, c0=c0, n=n: h.tensor_tensor(
                        out=tmps[:], in0=xT[:, :, c0:c0 + n], in1=bank[:, 0:n].unsqueeze(1).broadcast_to([128, 8, n]), op=ALU.mult)),
                        [t_rs] + tk_rel[0] + tk_rel[1])
                    t2 = P.op('dve', (lambda h: h.tensor_tensor(
                        out=tmps[:].rearrange("p k (s t) -> p k s t", t=8), in0=tmps[:].rearrange("p k (s t) -> p k s t", t=8),
                        in1=At[:, sub, :, 1:17].unsqueeze(3).broadcast_to([128, 8, 16, 8]), op=ALU.mult)), [t1] + mod_toks)
                    t3 = P.op('dve', (lambda h, c0=c0, n=n: h.tensor_tensor(
                        out=hdst[:, :, c0 - hcol0:c0 - hcol0 + n].rearrange("p k (s t) -> p k s t", t=8),
                        in0=tmps[:].rearrange("p k (s t) -> p k s t", t=8),
                        in1=modT[:, sub * 24:sub * 24 + 8, 1:17].unsqueeze(3).broadcast_to([128, 8, 16, 8]), op=ALU.add)), [t2])
                    tk_rel[0] = [t3]
                    tk_rel[1] = [t3]
                    toks.append(t3)
                PS.release(bi, toks if final else ([last_t1] if c0 < TP else toks[-1:]))
                htoks[(c0, n)] = toks
            return htoks

        gate_rel = []
        def resid_update(bank, i, c0, n, sub, mm, extra=()):
            if c0 < TP:
                t = P.op('dve', (lambda h, bank=bank, i=i, c0=c0, n=n: h.scalar_tensor_tensor(
                    out=xT[:, i, c0:c0 + n], in0=bank[:, 0:n], scalar=Gt[:, sub, i, 0:1], in1=xT[:, i, c0:c0 + n],
                    op0=ALU.mult, op1=ALU.add)), [mm] + list(extra))
                return t
            t1 = P.op('dve', (lambda h, bank=bank, i=i, n=n: h.tensor_tensor(
                out=gate_t[:, 0:n].rearrange("p (s t) -> p s t", t=8), in0=bank[:, 0:n].rearrange("p (s t) -> p s t", t=8),
                in1=Gt[:, sub, i, 1:17].unsqueeze(2).broadcast_to([128, 16, 8]), op=ALU.mult)), [mm] + list(extra) + gate_rel)
            t2 = P.op('dve', (lambda h, i=i, c0=c0, n=n: h.tensor_tensor(
                out=xT[:, i, c0:c0 + n], in0=xT[:, i, c0:c0 + n], in1=gate_t[:, 0:n], op=ALU.add)), [t1])
            gate_rel[:] = [t2]
            return t2

        def ffn(l, wgu_d, wdn_d, sub, htoks, hook=None):
            wgu = wgu_d[l]
            wdn = wdn_d[l]
            dn_rel = list(outs[-1:]) + [t for v in htoks.values() for t in v]
            g_readers = []
            sl_rel = [[], []]
            sidx = 0
            xlast = {}
            for grp in FF_GROUPS:
                gtok = {}
                for jj, j in enumerate(grp):
                    si, slot, wt = wslab([(wcols(wgu, j * 128), 0, 128), (wcols(wgu, DFF + j * 128), 128, 128)])
                    mm = None
                    for (c0, n) in TT_ALL:
                        ba, banka, da = PS.acquire()
                        bb, bankb, db = PS.acquire()
                        hd = htoks[(c0, n)]
                        for k in range(8):
                            mma = P.op('pe', (lambda h, banka=banka, slot=slot, k=k, c0=c0, n=n: h.matmul(
                                banka[:, 0:n], lhsT=slot[:, k, 0:128], rhs=hfull[:, k, c0:c0 + n], start=(k == 0), stop=(k == 7))),
                                [wt] + hd + da)
                        for k in range(8):
                            mm = P.op('pe', (lambda h, bankb=bankb, slot=slot, k=k, c0=c0, n=n: h.matmul(
                                bankb[:, 0:n], lhsT=slot[:, k, 128:256], rhs=hfull[:, k, c0:c0 + n], start=(k == 0), stop=(k == 7))),
                                [wt] + hd + db)
                        sb_ = sidx % 2
                        sidx += 1
                        ta = P.op('act', (lambda h, banka=banka, n=n, sb_=sb_: h.activation(out=silu_t[sb_][:, 0:n], in_=banka[:, 0:n], func=AF.Silu)),
                                  [mma] + sl_rel[sb_])
                        PS.release(ba, [ta])
                        tv = P.op('dve', (lambda h, bankb=bankb, jj=jj, c0=c0, n=n, sb_=sb_: h.tensor_tensor(
                            out=gbuf[:, jj, c0:c0 + n], in0=bankb[:, 0:n], in1=silu_t[sb_][:, 0:n], op=ALU.mult)), [mm, ta] + g_readers)
                        sl_rel[sb_] = [tv]
                        PS.release(bb, [tv])
                        gtok.setdefault((c0, n), []).append(tv)
                    RG.release(si, [mm])
                    if hook is not None:
                        hook()
                g_readers = []
                G = len(grp)
                j0 = grp[0]
                t_dn = P.dma('pool', (lambda h, G=G, j0=j0: h.dma_start(
                    out=dnbuf[:, 0:G, :], in_=wdn[j0 * 128:(j0 + G) * 128, :].rearrange("(j p) n -> p j n", p=128))), 'dn', dn_rel)
                mmd = None
                for i in range(8):
                    for (c0, n) in TT_ALL:
                        bi, bank, bdeps = PS.acquire()
                        for jj in range(G):
                            mmd = P.op('pe', (lambda h, bank=bank, jj=jj, i=i, c0=c0, n=n, G=G: h.matmul(
                                bank[:, 0:n], lhsT=dnbuf[:, jj, i * 128:(i + 1) * 128], rhs=gbuf[:, jj, c0:c0 + n],
                                start=(jj == 0), stop=(jj == G - 1))), [t_dn] + gtok[(c0, n)] + bdeps)
                        t = resid_update(bank, i, c0, n, sub, mmd, xlast.get((i, c0), []))
                        xlast[(i, c0)] = [t]
                        PS.release(bi, [t])
                dn_rel = [mmd]
                g_readers = [mmd]
            for (c0, n) in TT_ALL:
                set_xw(c0, n, [xlast[(i, c0)][0] for i in range(8)])

        def cwf(j, c):
            return vecs[:, 96 + j * 4 + c:96 + j * 4 + c + 1]

        def mixer_part(l, pi, tts, mod_toks, pre=None, next_tts=None):
            pc0 = tts[0][0]
            Th = sum(n for _, n in tts)
            win = w_in_d[l]
            has_sample = any(c0 >= TP for c0, _ in tts)
            ptts = [(c0, n) for (c0, n) in tts if c0 < TP]
            Tpp = sum(n for _, n in ptts)
            first = (pi == 0)
            if pre is None:
                P.barrier()
                htoks = norm(1, tts, hpart, pc0, DN_OFF, mod_toks)
                P.floor = []
            else:
                htoks = pre
            hall = [t for v in htoks.values() for t in v]

            import os
            MSTOP = int(os.environ.get("KMSTOP", "99"))
            if MSTOP <= 1:
                return
            qb = [av(TMP_OFF + 1544 * i, [128, 772], F32) for i in range(2)]
            xa_sb = [av(TMP_OFF + 3088 + 1024 * i, [128, 512], F32) for i in range(2)]
            cvt = [av(TMP_OFF + 5136 + 1536 * i, [128, 768], F32) for i in range(2)]
            xa_rel = [[], []]
            xi = 0
            hist_t = None
            if has_sample:
                hist_t = P.dma('sp', lambda h: h.dma_start(out=qs[:], in_=sconvT_d[l]), 'hist', P.now())
                hist_t = P.dma('sp', lambda h: h.dma_start(out=pss[:], in_=spoolT_d[l]), 'hist', P.now())
            za_toks = []
            qb_rel = [[], []]
            for c in range(4):
                q = qb[c % 2]
                cv = cvt[c % 2]
                if first:
                    th = P.op('dve', (lambda h, q=q: h.memset(q[:, 0:2], 0.0)), qb_rel[c % 2])
                else:
                    th = P.op('dve', (lambda h, q=q, c=c: h.tensor_copy(out=q[:, 0:2], in_=qhist[:, c, :])), qb_rel[c % 2])
                si, slot, wt = wslab([(wcols(win, c * 128), 0, 128), (wcols(win, 1024 + c * 128), 128, 128)])
                qtoks = []
                mm = None
                for (c0, n) in tts:
                    lc = c0 - pc0
                    ba, banka, da = PS.acquire()
                    bb, bankb, db = PS.acquire()
                    for k in range(8):
                        mma = P.op('pe', (lambda h, banka=banka, slot=slot, k=k, lc=lc, n=n: h.matmul(
                            banka[:, 0:n], lhsT=slot[:, k, 0:128], rhs=hpart[:, k, lc:lc + n], start=(k == 0), stop=(k == 7))), [wt] + hall + da)
                    for k in range(8):
                        mm = P.op('pe', (lambda h, bankb=bankb, slot=slot, k=k, lc=lc, n=n: h.matmul(
                            bankb[:, 0:n], lhsT=slot[:, k, 128:256], rhs=hpart[:, k, lc:lc + n], start=(k == 0), stop=(k == 7))), [wt] + hall + db)
                    xb_ = xi % 2
                    xi += 1
                    ta = P.op('act', (lambda h, banka=banka, n=n, xb_=xb_: h.activation(out=xa_sb[xb_][:, 0:n], in_=banka[:, 0:n], func=AF.Copy)),
                              [mma] + xa_rel[xb_])
                    PS.release(ba, [ta])
                    if c0 < TP:
                        tq = P.op('dve', (lambda h, bankb=bankb, q=q, lc=lc, n=n, xb_=xb_: h.tensor_tensor(
                            out=q[:, 2 + lc:2 + lc + n], in0=bankb[:, 0:n], in1=xa_sb[xb_][:, 0:n], op=ALU.mult)), [mm, ta, th])
                    else:
                        tq = P.op('dve', (lambda h, bankb=bankb, c=c, n=n, xb_=xb_: h.tensor_tensor(
                            out=qs[:, c, :, 2:10], in0=bankb[:, 0:n].rearrange("p (s t) -> p s t", t=8),
                            in1=xa_sb[xb_][:, 0:n].rearrange("p (s t) -> p s t", t=8), op=ALU.mult)), [mm, ta, hist_t])
                    xa_rel[xb_] = [tq]
                    PS.release(bb, [tq])
                    qtoks.append(tq)
                RG.release(si, [mm])
                ctoks = []
                if Tpp > 0:
                    t1 = P.op('dve', (lambda h, q=q, cv=cv, c=c: h.tensor_scalar(out=cv[:, 0:Tpp], in0=q[:, 2:2 + Tpp], scalar1=cwf(2, c), scalar2=None, op0=ALU.mult)),
                              qtoks + [lay_tok[l]])
                    t2 = P.op('dve', (lambda h, q=q, cv=cv, c=c: h.scalar_tensor_tensor(out=cv[:, 0:Tpp], in0=q[:, 1:1 + Tpp], scalar=cwf(1, c), in1=cv[:, 0:Tpp],
                                                                                     op0=ALU.mult, op1=ALU.add)), [t1])
                    t3 = P.op('dve', (lambda h, q=q, cv=cv, c=c: h.scalar_tensor_tensor(out=cv[:, 0:Tpp], in0=q[:, 0:Tpp], scalar=cwf(0, c), in1=cv[:, 0:Tpp],
                                                                                     op0=ALU.mult, op1=ALU.add)), [t2])
                    t4 = P.op('dve', (lambda h, q=q, c=c: h.tensor_copy(out=qhist[:, c, :], in_=q[:, Tpp:Tpp + 2])), [t3])
                    ctoks = [t3, t4]
                if has_sample:
                    cvs = cv[:, Tpp:Tpp + 128].rearrange("p (s t) -> p s t", t=8)
                    t1 = P.op('dve', (lambda h, cvs=cvs, c=c: h.tensor_scalar(out=cvs, in0=qs[:, c, :, 2:10], scalar1=cwf(2, c), scalar2=None, op0=ALU.mult)),
                              qtoks + [lay_tok[l]])
                    t2 = P.op('dve', (lambda h, cvs=cvs, c=c: h.scalar_tensor_tensor(out=cvs, in0=qs[:, c, :, 1:9], scalar=cwf(1, c), in1=cvs, op0=ALU.mult, op1=ALU.add)), [t1])
                    t3 = P.op('dve', (lambda h, cvs=cvs, c=c: h.scalar_tensor_tensor(out=cvs, in0=qs[:, c, :, 0:8], scalar=cwf(0, c), in1=cvs, op0=ALU.mult, op1=ALU.add)), [t2])
                    ctoks.append(t3)
                si, slot, wt = wslab([(wcols(win, 512 + c * 128), 0, 128)])
                mm = None
                last_z = []
                for (c0, n) in tts:
                    lc = c0 - pc0
                    bi, bank, bd = PS.acquire()
                    for k in range(8):
                        mm = P.op('pe', (lambda h, bank=bank, slot=slot, k=k, lc=lc, n=n: h.matmul(
                            bank[:, 0:n], lhsT=slot[:, k, 0:128], rhs=hpart[:, k, lc:lc + n], start=(k == 0), stop=(k == 7))), [wt] + hall + bd)
                    tz = P.op('dve', (lambda h, bank=bank, cv=cv, c=c, lc=lc, n=n: h.tensor_tensor(
                        out=zA[:, c, lc:lc + n], in0=bank[:, 0:n], in1=cv[:, lc:lc + n], op=ALU.mult)), [mm] + ctoks)
                    PS.release(bi, [tz])
                    last_z.append(tz)
                RG.release(si, [mm])
                qb_rel[c % 2] = last_z + ctoks
                za_toks += last_z
            if MSTOP <= 2:
                return

            ntile = Th // 128
            vg = av(DN_OFF, [128, 6, 512], F32)
            ub = [av(TMP_OFF + 8208 + 768 * i, [128, 768]) for i in range(2)]
            sgt = [av(DN_OFF + 6144 + 1024 * i, [128, 512], F32) for i in range(2)]
            stats = small[:, 0:36].rearrange("p (a b) -> p a b", a=6)
            mv = small[:, 36:48].rearrange("p (a b) -> p a b", a=6)
            rsd = small[:, 48:54]
            s1i, slot1, wt1 = wslab([(wcols(win, 2048, 256), 0, 256)])
            s2i, slot2, wt2 = wslab([(wcols(win, 2304, 256), 0, 256)])
            mm = None
            st_toks = []
            for ti in range(ntile):
                lt = ti * 128
                bi, bank, bd = PS.acquire()
                for hf, (slot, wt) in enumerate([(slot1, wt1), (slot2, wt2)]):
                    for k in range(8):
                        mm = P.op('pe', (lambda h, bank=bank, slot=slot, k=k, lt=lt, hf=hf: h.matmul(
                            bank[:, hf * 256:(hf + 1) * 256], lhsT=hpart[:, k, lt:lt + 128], rhs=slot[:, k, :], start=(k == 0), stop=(k == 7))),
                            [wt] + hall + bd)
                tg = P.op('act', (lambda h, bank=bank, ti=ti: h.activation(out=vg[:, ti, :], in_=bank[:], func=AF.Gelu)), [mm])
                PS.release(bi, [tg])
                ts = P.op('dve', (lambda h, ti=ti: h.bn_stats(out=stats[:, ti, :], in_=vg[:, ti, :])), [tg])
                ta = P.op('dve', (lambda h, ti=ti: h.bn_aggr(out=mv[:, ti, :], in_=stats[:, ti, :])), [ts])
                st_toks.append(ta)
            RG.release(s1i, [mm])
            RG.release(s2i, [mm])
            ub_rel = [[], []]

            def u_proj(c):
                u = ub[c % 2]
                si, slot, wt = wslab([(wcols(win, 1536 + c * 128), 0, 128)])
                utoks = []
                mmu = None
                for (c0, n) in tts:
                    lc = c0 - pc0
                    bi, bank, bd = PS.acquire()
                    for k in range(8):
                        mmu = P.op('pe', (lambda h, bank=bank, slot=slot, k=k, lc=lc, n=n: h.matmul(
                            bank[:, 0:n], lhsT=slot[:, k, 0:128], rhs=hpart[:, k, lc:lc + n], start=(k == 0), stop=(k == 7))), [wt] + hall + bd)
                    tu = P.op('act', (lambda h, bank=bank, u=u, lc=lc, n=n: h.activation(out=u[:, lc:lc + n], in_=bank[:, 0:n], func=AF.Gelu)), [mmu] + ub_rel[c % 2])
                    PS.release(bi, [tu])
                    utoks.append(tu)
                RG.release(si, [mmu])
                return utoks

            W_ = 15 + 768 + 1
            pb_ = [av(TMP_OFF + 1568 * i, [128, W_], F32) for i in range(3)]
            pt_ = [av(TMP_OFF + 4704 + 1568 * i, [128, W_], F32) for i in range(2)]
            fx = av(TMP_OFF + 7840, [128, 16], F32)
            pst = av(TMP_OFF + 7872, [128, 2, 16, 23], F32)
            pb_rel = [[], [], []]
            ppre = {}

            def p_proj(c):
                pbuf = pb_[c % 3]
                if first:
                    th = P.op('dve', (lambda h, pbuf=pbuf: h.memset(pbuf[:, 0:15], 0.0)), pb_rel[c % 3] + za_toks)
                else:
                    th = P.op('dve', (lambda h, pbuf=pbuf, c=c: h.tensor_copy(out=pbuf[:, 0:15], in_=phist[:, c, :])), pb_rel[c % 3] + za_toks)
                si, slot, wt = wslab([(wcols(win, 2560 + c * 128), 0, 128)])
                ptoks = []
                mmp = None
                for (c0, n) in tts:
                    lc = c0 - pc0
                    bi, bank, bd = PS.acquire()
                    for k in range(8):
                        mmp = P.op('pe', (lambda h, bank=bank, slot=slot, k=k, lc=lc, n=n: h.matmul(
                            bank[:, 0:n], lhsT=slot[:, k, 0:128], rhs=hpart[:, k, lc:lc + n], start=(k == 0), stop=(k == 7))), [wt] + hall + bd)
                    if c0 < TP:
                        tp = P.op('act', (lambda h, bank=bank, pbuf=pbuf, lc=lc, n=n: h.activation(out=pbuf[:, 15 + lc:15 + lc + n], in_=bank[:, 0:n], func=AF.Copy)), [mmp, th])
                    else:
                        tp = P.op('act', (lambda h, bank=bank, c=c, n=n: h.activation(out=pss[:, c, :, 15:23], in_=bank[:, 0:n].rearrange("p (s t) -> p s t", t=8), func=AF.Copy)),
                                  [mmp, hist_t])
                    PS.release(bi, [tp])
                    ptoks.append(tp)
                RG.release(si, [mmp])
                return pbuf, ptoks

            upre = {0: u_proj(0), 1: u_proj(1)}
            for c_ in range(3):
                ppre[c_] = p_proj(c_)
            t_sq = P.op('act', (lambda h: h.activation(out=rsd[:, 0:ntile], in_=mv[:, 0:ntile, 1], func=AF.Sqrt, scale=1.0, bias=epsb[:, 0:1])), st_toks + [t_eps])
            t_rc = P.op('dve', (lambda h: h.reciprocal(out=rsd[:, 0:ntile], in_=rsd[:, 0:ntile])), [t_sq])
            vn_toks = []
            for ti in range(ntile):
                t1 = P.op('dve', (lambda h, ti=ti: h.tensor_scalar(out=vg[:, ti, :], in0=vg[:, ti, :], scalar1=mv[:, ti, 0:1], scalar2=rsd[:, ti:ti + 1],
                                                                 op0=ALU.subtract, op1=ALU.mult)), [t_rc])
                is_s = has_sample and ti == ntile - 1
                if is_s:
                    t2 = P.op('dve', (lambda h, ti=ti: h.tensor_tensor(out=vg[:, ti, :], in0=vg[:, ti, :], in1=lng[:], op=ALU.mult)), [t1, lay_tok[l]])
                    t3 = P.op('dve', (lambda h, ti=ti: h.tensor_copy(out=vn[:, ti, :], in_=vg[:, ti, :])), [t2])
                    nvs_tok = [P.dma('sp', (lambda h, ti=ti: h.dma_start(out=nvs_d[l], in_=vg[:, ti, :])), 'nvs', [t2])]
                    fin_extra.append(nvs_tok[0])
                    vn_toks.append(t3)
                else:
                    t2 = P.op('dve', (lambda h, ti=ti: h.tensor_tensor(out=vn[:, ti, :], in0=vg[:, ti, :], in1=lng[:], op=ALU.mult)), [t1, lay_tok[l]])
                    vn_toks.append(t2)
            sg_rel = [[], []]
            sgi = 0
            zb_toks = []
            sg_last = None
            for c in range(4):
                u = ub[c % 2]
                utoks = upre[c]
                ztk = []
                for (c0, n) in tts:
                    lc = c0 - pc0
                    bi, bank, bd = PS.acquire()
                    mm = None
                    for tj in range(n // 128):
                        ti = (lc + tj * 128) // 128
                        is_s = (c0 >= TP)
                        wsrc = ws8 if is_s else wsT
                        for hh in range(2):
                            hd = 2 * c + hh
                            mm = P.op('pe', (lambda h, bank=bank, ti=ti, tj=tj, hh=hh, hd=hd, wsrc=wsrc: h.matmul(
                                bank[hh * 64:(hh + 1) * 64, tj * 128:(tj + 1) * 128], lhsT=vn[:, ti, hd * 64:(hd + 1) * 64],
                                rhs=wsrc[:, hd, :], start=True, stop=True)), vn_toks + [lay_tok[l]] + bd)
                    sg_last = mm
                    sb_ = sgi % 2
                    sgi += 1
                    bsrc = biass if c0 >= TP else biasp
                    t1 = P.op('dve', (lambda h, bank=bank, n=n, sb_=sb_, bsrc=bsrc, c=c: h.tensor_tensor(
                        out=sgt[sb_][:, 0:n].rearrange("p (a b) -> p a b", b=128), in0=bank[:, 0:n].rearrange("p (a b) -> p a b", b=128),
                        in1=bsrc[:, c:c + 1, :].broadcast_to([128, n // 128, 128]), op=ALU.add)), [mm, lay_tok[l]] + sg_rel[sb_])
                    PS.release(bi, [t1])
                    t2 = P.op('dve', (lambda h, u=u, c=c, lc=lc, n=n, sb_=sb_: h.tensor_tensor(
                        out=zB[:, c, lc:lc + n], in0=sgt[sb_][:, 0:n], in1=u[:, lc:lc + n], op=ALU.mult)), [t1] + utoks)
                    sg_rel[sb_] = [t2]
                    ztk.append(t2)
                ub_rel[c % 2] = ztk
                if c + 2 < 4:
                    upre[c + 2] = u_proj(c + 2)
                zb_toks += ztk
            if MSTOP <= 3:
                return

            zc_toks = []
            for c in range(4):
                w = WINS[c]
                if c not in ppre:
                    ppre[c] = p_proj(c)
                pbuf, ptoks = ppre[c]
                ztk = []
                if Tpp > 0:
                    L = 15 + Tpp
                    src = pbuf
                    sh = 1
                    tprev = ptoks
                    lvl = 0
                    while sh < w:
                        dst = pt_[lvl % 2]
                        tprev = [P.op('dve', (lambda h, src=src, dst=dst, sh=sh, L=L: h.tensor_tensor(
                            out=dst[:, sh:L], in0=src[:, sh:L], in1=src[:, 0:L - sh], op=ALU.add)), tprev)]
                        src = dst
                        sh *= 2
                        lvl += 1
                    tz = P.op('dve', (lambda h, src=src, pbuf=pbuf, c=c, w=w: h.scalar_tensor_tensor(
                        out=zC[:, c, 0:Tpp], in0=src[:, 15:15 + Tpp], scalar=1.0 / w, in1=pbuf[:, 15:15 + Tpp], op0=ALU.mult, op1=ALU.subtract)), tprev)
                    ztk.append(tz)
                    if first:
                        tf1 = P.op('dve', (lambda h, src=src, c=c, w=w: h.tensor_tensor(out=fx[:, 0:w - 1], in0=src[:, 15:15 + w - 1], in1=invcnt[:, c, 0:w - 1], op=ALU.mult)),
                                   [tz, t_cst])
                        tf2 = P.op('dve', (lambda h, pbuf=pbuf, c=c, w=w: h.tensor_tensor(out=zC[:, c, 0:w - 1], in0=fx[:, 0:w - 1], in1=pbuf[:, 15:15 + w - 1], op=ALU.subtract)), [tf1])
                        ztk.append(tf2)
                    th2 = P.op('dve', (lambda h, pbuf=pbuf, c=c: h.tensor_copy(out=phist[:, c, :], in_=pbuf[:, Tpp:Tpp + 15])), ztk)
                    ztk.append(th2)
                if has_sample:
                    srcv = pss[:, c, :, :]
                    sh = 1
                    tprev = ptoks
                    lvl = 0
                    while sh < w:
                        dstv = pst[:, lvl % 2, :, :]
                        tprev = [P.op('dve', (lambda h, srcv=srcv, dstv=dstv, sh=sh: h.tensor_tensor(
                            out=dstv[:, :, sh:23], in0=srcv[:, :, sh:23], in1=srcv[:, :, 0:23 - sh], op=ALU.add)), tprev + ztk[-1:])]
                        srcv = dstv
                        sh *= 2
                        lvl += 1
                    tz = P.op('dve', (lambda h, srcv=srcv, c=c, w=w: h.scalar_tensor_tensor(
                        out=zC[:, c, Tpp:Tpp + 128].rearrange("p (s t) -> p s t", t=8), in0=srcv[:, :, 15:23], scalar=1.0 / w,
                        in1=pss[:, c, :, 15:23], op0=ALU.mult, op1=ALU.subtract)), tprev)
                    ztk.append(tz)
                pb_rel[c % 3] = ztk
                zc_toks += ztk
            if MSTOP <= 4:
                return

            if has_sample:
                qtok = [av(TMP_OFF + 1024 * i, [128, 512], F32) for i in range(2)]
                ptok = [av(TMP_OFF + 2048 + 1024 * i, [128, 512], F32) for i in range(2)]
                xat = [av(TMP_OFF + 4096 + 512 * i, [128, 256], F32) for i in range(2)]
                xat_rel = [[], []]
                xi = 0
                tiles = [(Th - 256, 0), (Th - 128, 1)]
                qt_toks = [[], []]
                pt_toks = [[], []]
                for hf in range(2):
                    s1i, slot1, wt1 = wslab([(wcols(win, hf * 256, 256), 0, 256)])
                    s2i, slot2, wt2 = wslab([(wcols(win, 1024 + hf * 256, 256), 0, 256)])
                    s3i, slot3, wt3 = wslab([(wcols(win, 2560 + hf * 256, 256), 0, 256)])
                    mm = None
                    for (lt, idx) in tiles:
                        bi, bank, bd = PS.acquire()
                        for k in range(8):
                            mma = P.op('pe', (lambda h, bank=bank, k=k, lt=lt, slot1=slot1: h.matmul(bank[:, 0:256], lhsT=hpart[:, k, lt:lt + 128], rhs=slot1[:, k, :],
                                                                                      start=(k == 0), stop=(k == 7))), [wt1] + hall + bd)
                        for k in range(8):
                            mm = P.op('pe', (lambda h, bank=bank, k=k, lt=lt, slot2=slot2: h.matmul(bank[:, 256:512], lhsT=hpart[:, k, lt:lt + 128], rhs=slot2[:, k, :],
                                                                                     start=(k == 0), stop=(k == 7))), [wt2] + hall)
                        xb_ = xi % 2
                        xi += 1
                        ta = P.op('act', (lambda h, bank=bank, xb_=xb_: h.activation(out=xat[xb_][:], in_=bank[:, 0:256], func=AF.Copy)), [mma, mm] + xat_rel[xb_] + zc_toks)
                        tq = P.op('dve', (lambda h, bank=bank, xb_=xb_, idx=idx, hf=hf: h.tensor_tensor(
                            out=qtok[idx][:, hf * 256:(hf + 1) * 256], in0=bank[:, 256:512], in1=xat[xb_][:], op=ALU.mult)), [mm, ta])
                        xat_rel[xb_] = [tq]
                        PS.release(bi, [tq])
                        qt_toks[idx].append(tq)
                        bi, bank, bd = PS.acquire()
                        for k in range(8):
                            mm = P.op('pe', (lambda h, bank=bank, k=k, lt=lt, slot3=slot3: h.matmul(bank[:, 0:256], lhsT=hpart[:, k, lt:lt + 128], rhs=slot3[:, k, :],
                                                                                     start=(k == 0), stop=(k == 7))), [wt3] + hall + bd)
                        tp = P.op('act', (lambda h, bank=bank, idx=idx, hf=hf: h.activation(out=ptok[idx][:, hf * 256:(hf + 1) * 256], in_=bank[:, 0:256], func=AF.Copy)), [mm] + zc_toks)
                        PS.release(bi, [tp])
                        pt_toks[idx].append(tp)
                    RG.release(s1i, [mm]); RG.release(s2i, [mm]); RG.release(s3i, [mm])
                for j in range(2):
                    outs.append(P.dma('sp', (lambda h, j=j: h.dma_start(out=ncp_d[l, j:j + 1, :], in_=qtok[0][126 + j:127 + j, :])), 'out', qt_toks[0]))
                for j in range(15):
                    outs.append(P.dma('sp', (lambda h, j=j: h.dma_start(out=npp_d[l, j:j + 1, :], in_=ptok[0][113 + j:114 + j, :])), 'out', pt_toks[0]))
                outs.append(P.dma('sp', lambda h: h.dma_start(out=ncs_d[l, :, 0, :], in_=qtok[1][6:128:8, :]), 'out', qt_toks[1]))
                outs.append(P.dma('sp', lambda h: h.dma_start(out=ncs_d[l, :, 1, :], in_=qtok[1][7:128:8, :]), 'out', qt_toks[1]))
                bt = av(TMP_OFF + 5120, [128, 512], F32)
                bguard = P.now()
                lt_ = None
                for r in range(7):
                    lt_ = P.dma('sp', (lambda h, r=r: h.dma_start(out=bt[r * 16:(r + 1) * 16, :], in_=spool_raw_d[l, :, 8 + r, :])), 'bt', bguard)
                for r in range(7):
                    outs.append(P.dma('sp', (lambda h, r=r: h.dma_start(out=nps_d[l, :, r, :], in_=bt[r * 16:(r + 1) * 16, :])), 'out', [lt_]))
                for t8 in range(8):
                    outs.append(P.dma('sp', (lambda h, t8=t8: h.dma_start(out=nps_d[l, :, 7 + t8, :], in_=ptok[1][t8:128:8, :])), 'out', pt_toks[1]))

            if MSTOP <= 5:
                return
            sig = [[av(DN_OFF + (g * 2 + i) * 1024, [128, 512], F32) for i in range(2)] for g in range(3)]
            tt_ = [av(VN_OFF + i * 1024, [128, 512], F32) for i in range(3)]
            sig_rel = [[[], []] for _ in range(3)]
            tt_rel = []
            si_ = 0
            mg_toks = []
            zall = za_toks + zb_toks + zc_toks
            woa = woa_d[l]
            wob = wob_d[l]
            for i in range(8):
                s1i, slot1, wt1 = wslab([(wcols(win, 3072 + i * 128), 0, 128), (wcols(win, 4096 + i * 128), 128, 128)])
                s2i, slot2, wt2 = wslab([(wcols(win, 5120 + i * 128), 0, 128),
                                         (woa[:, i * 128:(i + 1) * 128].rearrange("(k p) n -> p k n", p=128), 128, 128, 0, 4),
                                         (wob[:, i * 128:(i + 1) * 128].rearrange("(k p) n -> p k n", p=128), 128, 128, 4, 8)])
                mm = None
                gate_toks = {}
                for tix, (c0, n) in enumerate(tts):
                    lc = c0 - pc0
                    sb_ = tix % 2
                    banks = []
                    for g in range(3):
                        bi, bank, bd = PS.acquire()
                        slot, off, wt = [(slot1, 0, wt1), (slot1, 128, wt1), (slot2, 0, wt2)][g]
                        for k in range(8):
                            mm = P.op('pe', (lambda h, bank=bank, slot=slot, off=off, k=k, lc=lc, n=n: h.matmul(
                                bank[:, 0:n], lhsT=slot[:, k, off:off + 128], rhs=hpart[:, k, lc:lc + n], start=(k == 0), stop=(k == 7))), [wt] + hall + bd)
                        ts = P.op('act', (lambda h, bank=bank, g=g, sb_=sb_, n=n: h.activation(out=sig[g][sb_][:, 0:n], in_=bank[:, 0:n], func=AF.Sigmoid)),
                                  [mm] + sig_rel[g][sb_] + vn_toks + zb_toks + (nvs_tok if has_sample else []))
                        PS.release(bi, [ts])
                        banks.append(ts)
                    gate_toks[tix] = banks
                for tix, (c0, n) in enumerate(tts):
                    lc = c0 - pc0
                    sb_ = tix % 2
                    banks = gate_toks[tix]
                    ys = []
                    for g in range(3):
                        bi, bank, bd = PS.acquire()
                        if g < 2:
                            zz = zA if g == 0 else zB
                            for k in range(4):
                                mm = P.op('pe', (lambda h, bank=bank, zz=zz, g=g, k=k, lc=lc, n=n, slot2=slot2: h.matmul(
                                    bank[:, 0:n], lhsT=slot2[:, g * 4 + k, 128:256], rhs=zz[:, k, lc:lc + n], start=(k == 0), stop=(k == 3))), [wt2] + (za_toks if g == 0 else zb_toks) + bd)
                        else:
                            gq = i // 2
                            mm = P.op('pe', (lambda h, bank=bank, gq=gq, i=i, lc=lc, n=n: h.matmul(
                                bank[:, 0:n], lhsT=poolw[:, gq, (i % 2) * 128:(i % 2) * 128 + 128], rhs=zC[:, gq, lc:lc + n], start=True, stop=True)),
                                [lay_tok[l]] + zc_toks + bd)
                        ys.append((bi, bank, mm))
                    t1 = P.op('dve', (lambda h, bank=ys[0][1], sb_=sb_, n=n: h.tensor_tensor(out=tt_[0][:, 0:n], in0=bank[:, 0:n], in1=sig[0][sb_][:, 0:n], op=ALU.mult)),
                              [ys[0][2], banks[0], sg_last] + tt_rel)
                    PS.release(ys[0][0], [t1])
                    t2 = P.op('dve', (lambda h, bank=ys[1][1], sb_=sb_, n=n: h.tensor_tensor(out=tt_[1][:, 0:n], in0=bank[:, 0:n], in1=sig[1][sb_][:, 0:n], op=ALU.mult)),
                              [ys[1][2], banks[1], sg_last])
                    PS.release(ys[1][0], [t2])
                    t3 = P.op('dve', (lambda h, bank=ys[2][1], sb_=sb_, n=n, i=i: h.scalar_tensor_tensor(
                        out=tt_[2][:, 0:n], in0=bank[:, 0:n], scalar=vecs[:, 108 + i:109 + i], in1=sig[2][sb_][:, 0:n], op0=ALU.mult, op1=ALU.mult)),
                        [ys[2][2], banks[2], lay_tok[l], sg_last])
                    PS.release(ys[2][0], [t3])
                    t4 = P.op('dve', (lambda h, n=n: h.tensor_tensor(out=tt_[0][:, 0:n], in0=tt_[0][:, 0:n], in1=tt_[1][:, 0:n], op=ALU.add)), [t1, t2])
                    t5 = P.op('dve', (lambda h, i=i, lc=lc, n=n: h.tensor_tensor(out=merged[:, i, lc:lc + n], in0=tt_[0][:, 0:n], in1=tt_[2][:, 0:n], op=ALU.add)), [t4, t3])
                    for g in range(3):
                        sig_rel[g][sb_] = [t5]
                    tt_rel = [t5]
                    mg_toks.append(t5)
                RG.release(s1i, [mm]); RG.release(s2i, [mm])
            if MSTOP <= 6:
                return
            nxt = None
            if next_tts is not None:
                nxt = norm(1, next_tts, hpart, next_tts[0][0], DN_OFF, mod_toks, extra=mg_toks)
            wo = wo_d[l]
            xl = {}
            for ip in range(8):
                if ip % 2 == 0:
                    si, slot, wt = wslab([(wcols(wo, ip * 128, 256), 0, 256)])
                mm = None
                for (c0, n) in tts:
                    lc = c0 - pc0
                    bi, bank, bd = PS.acquire()
                    for k in range(8):
                        mm = P.op('pe', (lambda h, bank=bank, slot=slot, ip=ip, k=k, lc=lc, n=n: h.matmul(
                            bank[:, 0:n], lhsT=slot[:, k, (ip % 2) * 128:(ip % 2) * 128 + 128], rhs=merged[:, k, lc:lc + n], start=(k == 0), stop=(k == 7))),
                            [wt] + mg_toks + bd)
                    t = resid_update(bank, ip, c0, n, 1, mm, mod_toks)
                    PS.release(bi, [t])
                    xl.setdefault(c0, []).append(t)
                if ip % 2 == 1:
                    RG.release(si, [mm])
            for (c0, n) in tts:
                set_xw(c0, n, xl[c0])
            return nxt

        import os
        STAGE = int(os.environ.get("KSTAGE", "99"))
        outs = []
        fin_extra = []
        if STAGE >= 1:
            load_layer_small(0)
            mj = ModJob(0)
        for l in range(nl):
            if l == 0:
                for _ in range(6):
                    mj.step()
                mj._flush()
                mod_toks = finish_mod(l, list(mj.toks), subs=(0,))
                hook1 = mj.step
            else:
                mod_toks = finish_mod(l, mj.finish())
                hook1 = None
            P.barrier()
            htoks = norm(0, TT_ALL, hfull, 0, DN_OFF, mod_toks)
            P.floor = []
            ffn(l, w1gu_d, w1dn_d, 0, htoks, hook1)
            if l == 0:
                mod_toks = mod_toks + finish_mod(l, mj.finish(), subs=(1, 2))
            for pi, tts in enumerate(PARTS):
                if STAGE == 4 and pi >= 1:
                    break
                pre = mixer_part(l, pi, tts, mod_toks, pre if pi > 0 else None, PARTS[pi + 1] if pi + 1 < len(PARTS) else None)
            if STAGE < 6:
                break
            P.barrier()
            mix_end = P.now()
            htoks = norm(2, TT_ALL, hfull, 0, DN_OFF, mod_toks)
            P.floor = []
            hook = None
            if l + 1 < nl:
                load_layer_small(l + 1, mix_end)
                mj = ModJob(l + 1)
                mj.extra = [t for v in htoks.values() for t in v]
                hook = mj.step
            ffn(l, w2gu_d, w2dn_d, 2, htoks, hook)
        P.barrier()
        norm(0, TT_ALL, None, 0, DN_OFF, [], final=True)
        P.barrier()
        ostage = [av(H_OFF + 2048 * i, [128, 1024], F32) for i in range(2)]
        ost_rel = [[], []]
        for ti in range(17):
            b = ti % 2
            cps = []
            for a in range(2):
                bi, bank, bd = PS.acquire()
                tr = None
                for j in range(4):
                    dc = a * 4 + j
                    tr = P.op('pe', (lambda h, bank=bank, dc=dc, j=j, ti=ti: h.transpose(
                        out=bank[:, j * 128:(j + 1) * 128], in_=xT[:, dc, ti * 128:(ti + 1) * 128], identity=ident[:])), xdeps(ti * 128, 128) + bd)
                if a == 0:
                    tc = P.op('act', (lambda h, bank=bank, b=b, a=a: h.activation(out=ostage[b][:, a * 512:(a + 1) * 512], in_=bank[:], func=AF.Copy)), [tr] + ost_rel[b])
                else:
                    tc = P.op('dve', (lambda h, bank=bank, b=b, a=a: h.tensor_copy(out=ostage[b][:, a * 512:(a + 1) * 512], in_=bank[:])), [tr] + ost_rel[b])
                PS.release(bi, [tc])
                cps.append(tc)
            dst = yp_d[ti * 128:(ti + 1) * 128, :] if ti < 16 else ys_d
            to = P.dma('sp', (lambda h, b=b, dst=dst: h.dma_start(out=dst, in_=ostage[b])), f"stg{b}", cps)
            ost_rel[b] = [to]
            fin_toks = [t for t in ost_rel[0] + ost_rel[1]]
        for e in ENGS:
            P.wait(e, fin_toks + outs[-1:] + fin_extra[-1:])
        P.emit(nc, sems)
    return nc


_NC = {}


def _host_prep(inp, nl=NL):
    f = lambda a: np.ascontiguousarray(np.asarray(a, dtype=np.float32))
    x_prompt = f(inp['x_prompt']); x_sample = f(inp['x_sample'])
    state_conv = f(inp['state_conv']); state_pool = f(inp['state_pool'])
    c_prompt = f(inp['c_prompt']); c_sample = f(inp['c_sample'])
    norm_g = f(inp['norm_g']); b_ada = f(inp['b_ada']); conv_w = f(inp['conv_w']); pool_scale = f(inp['pool_scale'])
    b_s = f(inp['b_s']); w_s = f(inp['w_s'])
    vecs = np.zeros((NL, 128, 116), np.float32)
    biasp = np.zeros((NL, 128, 4, 128), np.float32)
    biass = np.zeros((NL, 128, 4, 128), np.float32)
    wsT = np.zeros((NL, 128, 8, 128), np.float32)
    ws8 = np.zeros((NL, 128, 8, 128), np.float32)
    for l in range(NL):
        vecs[l, :, 0:72] = b_ada[l].reshape(72, 128).T
        vecs[l, :, 72:96] = norm_g[l].reshape(24, 128).T
        vecs[l, :, 96:108] = conv_w[l].reshape(12, 128).T
        vecs[l, :, 108:116] = pool_scale[l].reshape(8, 128).T
        bp = b_s[l].reshape(4, 2, 128).transpose(1, 0, 2)
        biasp[l] = np.repeat(bp[:, None, :, :], 64, axis=1).reshape(128, 4, 128)
        biass[l] = np.tile(biasp[l][:, :, :8], (1, 1, 16))
        wsT[l] = w_s[l].transpose(2, 0, 1)
        sm = w_s[l][:, :8, :8].transpose(2, 0, 1)
        ws8[l] = np.tile(sm, (16, 1, 16))
    p = np.arange(128)
    masks = np.zeros((128, 2, 128), np.float32)
    masks[:, 0, :] = (p[:, None] <= p[None, :])
    masks[:, 1, :] = ((p[:, None] // 8) == (p[None, :] // 8)) & ((p[:, None] % 8) <= (p[None, :] % 8))
    invcnt = np.zeros((128, 4, 16), np.float32)
    for g, w in enumerate(WINS):
        invcnt[:, g, :] = 1.0 / np.minimum(np.arange(16) + 1, w)
    shared = dict(
        vecs=vecs, fin=f(inp['final_norm_g']).reshape(8, 128).T.copy(), lng=f(inp['ln_g']),
        biasp=biasp, biass=biass, wsT=wsT, ws8=ws8, masks=masks, invcnt=invcnt, ident=np.eye(128, dtype=np.float32),
        w_ada=f(inp['w_ada']), w1_gu=f(inp['w1_gu']), w1_dn=f(inp['w1_dn']), w2_gu=f(inp['w2_gu']), w2_dn=f(inp['w2_dn']),
        w_in=f(inp['w_in']), w_out_a=f(inp['w_out_a']), w_out_b=f(inp['w_out_b']), pool_w=f(inp['pool_w']), w_o=f(inp['w_o']),
    )
    in_maps = []
    for c in range(8):
        sc = state_conv[:, c * 16:(c + 1) * 16]
        sp = state_pool[:, c * 16:(c + 1) * 16]
        c_all = np.concatenate([c_prompt[c:c + 1], c_sample[c * 16:(c + 1) * 16]], 0)
        m = dict(shared)
        m['xp'] = np.ascontiguousarray(x_prompt[c])
        m['xs'] = np.ascontiguousarray(x_sample[c * 16:(c + 1) * 16].reshape(128, D))
        m['cT'] = np.ascontiguousarray(c_all.T.reshape(8, 128, 17).transpose(1, 0, 2))
        scT_ = np.zeros((NL, 128, 4, 16, 10), np.float32)
        scT_[..., 0:2] = sc.reshape(NL, 16, 2, 4, 128).transpose(0, 4, 3, 1, 2)
        spT_ = np.zeros((NL, 128, 4, 16, 23), np.float32)
        spT_[..., 0:15] = sp.reshape(NL, 16, 15, 4, 128).transpose(0, 4, 3, 1, 2)
        m['sconvT'] = scT_
        m['spoolT'] = spT_
        m['spool_raw'] = np.ascontiguousarray(sp)
        in_maps.append(m)
    return in_maps


def kernel(**inp):
    if 'nc' not in _NC:
        _NC['nc'] = build(NL)
    nc = _NC['nc']
    in_maps = _host_prep(inp)
    res = run_bass_kernel_spmd(nc, in_maps, core_ids=list(range(8)))
    r = res.results
    y_prompt = np.stack([r[c]['yp'] for c in range(8)], 0)
    y_sample = np.concatenate([r[c]['ys'].reshape(16, 8, D) for c in range(8)], 0)
    ncp = np.stack([r[c]['ncp'] for c in range(8)], 1)
    ncs = np.concatenate([r[c]['ncs'] for c in range(8)], 1)
    npp = np.stack([r[c]['npp'] for c in range(8)], 1)
    nps = np.concatenate([r[c]['nps'] for c in range(8)], 1)
    nvs = np.concatenate([r[c]['nvs'].reshape(NL, 16, 8, 512) for c in range(8)], 1)
    return (y_prompt.astype(np.float32), y_sample.astype(np.float32), ncp.astype(np.float32), ncs.astype(np.float32),
            npp.astype(np.float32), nps.astype(np.float32), nvs.astype(np.float32))
```

```python
import numpy as np
from contextlib import ExitStack
import concourse.bass as bass
import concourse.mybir as mybir
from concourse.bass_utils import run_bass_kernel_spmd

F32 = mybir.dt.float32
BF16 = mybir.dt.bfloat16
AF = mybir.ActivationFunctionType
ALU = mybir.AluOpType
ENGS = ['pe', 'act', 'dve', 'pool', 'sp']
CENGS = ['pe', 'act', 'dve', 'pool']

D = 1024
NL = 4
T = 2176
TP = 2048
DFF = 2816
NFF = 22
EPS = 1e-6
TT_ALL = [(0, 512), (512, 512), (1024, 512), (1536, 512), (2048, 128)]
PARTS = [[(0, 512), (512, 256)], [(768, 512), (1280, 256)], [(1536, 512), (2048, 128)]]
FF_GROUPS = [list(range(0, 8)), list(range(8, 16)), list(range(16, 22))]
NSLOT = 4
WINS = (2, 4, 8, 16)


class Prog:
    def __init__(self):
        self.q = {e: [] for e in ENGS}
        self.cnt = {}
        self.seen = {e: {} for e in ENGS}
        self.floor = []

    def _waits(self, eng, deps, use_floor=True):
        waits = []
        for d in list(deps) + (self.floor if use_floor else []):
            if d is None:
                continue
            name, v = d
            if name == eng and False:
                continue
            if v > self.seen[eng].get(name, 0):
                self.seen[eng][name] = v
                waits.append((name, v))
        return waits

    def op(self, eng, fn, deps=()):
        waits = self._waits(eng, deps)
        self.cnt[eng] = self.cnt.get(eng, 0) + 1
        self.q[eng].append((waits, fn, (eng, 1)))
        return (eng, self.cnt[eng])

    def dma(self, eng, fn, sem, deps=()):
        waits = self._waits(eng, deps, use_floor=False)
        self.cnt[sem] = self.cnt.get(sem, 0) + 16
        self.q[eng].append((waits, fn, (sem, 16)))
        return (sem, self.cnt[sem])

    def wait(self, eng, deps):
        waits = self._waits(eng, deps)
        if waits:
            self.q[eng].append((waits, None, None))

    def now(self):
        return [(e, self.cnt[e]) for e in CENGS if self.cnt.get(e, 0) > 0]

    def barrier(self):
        self.floor = self.now()

    def emit(self, nc, sems):
        with nc.Block() as block:
            def run(eng_name):
                def f(h):
                    for waits, fn, inc in self.q[eng_name]:
                        for (name, v) in waits:
                            h.wait_ge(sems[name], v)
                        if fn is not None:
                            ins = fn(h)
                            ins.then_inc(sems[inc[0]], inc[1])
                return f
            block.tensor(run('pe'))
            block.scalar(run('act'))
            block.vector(run('dve'))
            block.gpsimd(run('pool'))
            block.sync(run('sp'))


class Ring:
    def __init__(self, bufs):
        self.bufs = bufs
        self.rel = [[] for _ in bufs]
        self.i = 0

    def acquire(self):
        i = self.i
        self.i = (self.i + 1) % len(self.bufs)
        deps = self.rel[i]
        self.rel[i] = []
        return i, self.bufs[i], deps

    def release(self, i, toks):
        self.rel[i] = list(toks)


def build(nl=NL):
    nc = bass.Bass("TRN2", target_bir_lowering=False)
    P = Prog()

    def din(name, shape):
        return nc.dram_tensor(name, list(shape), F32, kind="ExternalInput").ap()

    def dout(name, shape):
        return nc.dram_tensor(name, list(shape), F32, kind="ExternalOutput").ap()

    xp_d = din("xp", [TP, D]); xs_d = din("xs", [128, D]); cT_d = din("cT", [128, 8, 17])
    sconvT_d = din("sconvT", [NL, 128, 4, 16, 10]); spoolT_d = din("spoolT", [NL, 128, 4, 16, 23])
    spool_raw_d = din("spool_raw", [NL, 16, 15, 512])
    vecs_d = din("vecs", [NL, 128, 116]); fin_d = din("fin", [128, 8]); lng_d = din("lng", [NL, 512])
    biasp_d = din("biasp", [NL, 128, 4, 128]); biass_d = din("biass", [NL, 128, 4, 128])
    wsT_d = din("wsT", [NL, 128, 8, 128]); ws8_d = din("ws8", [NL, 128, 8, 128])
    masks_d = din("masks", [128, 2, 128]); invcnt_d = din("invcnt", [128, 4, 16]); ident_d = din("ident", [128, 128])
    w_ada_d = din("w_ada", [NL, D, 9216]); w1gu_d = din("w1_gu", [NL, D, 2 * DFF]); w1dn_d = din("w1_dn", [NL, DFF, D])
    w2gu_d = din("w2_gu", [NL, D, 2 * DFF]); w2dn_d = din("w2_dn", [NL, DFF, D]); w_in_d = din("w_in", [NL, D, 6144])
    woa_d = din("w_out_a", [NL, 512, D]); wob_d = din("w_out_b", [NL, 512, D]); poolw_d = din("pool_w", [NL, 4, 128, 256])
    wo_d = din("w_o", [NL, D, D])
    yp_d = dout("yp", [TP, D]); ys_d = dout("ys", [128, D])
    ncp_d = dout("ncp", [NL, 2, 512]); ncs_d = dout("ncs", [NL, 16, 2, 512])
    npp_d = dout("npp", [NL, 15, 512]); nps_d = dout("nps", [NL, 16, 15, 512]); nvs_d = dout("nvs", [NL, 128, 512])

    with ExitStack() as es:
        def sb(name, shape, dt):
            return es.enter_context(nc.sbuf_tensor('s_' + name, list(shape), dt))

        xT = sb("xT", [128, 8, T], F32)
        ARENA = 45568
        arena = sb("arena", [128, ARENA], BF16)
        ring_t = [sb(f"ring{i}", [128, 8, 256], BF16) for i in range(NSLOT)]
        ident = sb("ident", [128, 128], F32)
        ones = sb("ones", [128, 128], BF16)
        masks = sb("masks", [128, 2, 128], F32)
        invcnt = sb("invcnt", [128, 4, 16], F32)
        fin = sb("fin", [128, 8], F32)
        scT = sb("scT", [128, 8, 17], BF16)
        vecs = sb("vecs", [128, 116], F32)
        lng = sb("lngbc", [128, 512], F32)
        biasp = sb("biasp", [128, 4, 128], F32)
        biass = sb("biass", [128, 4, 128], F32)
        wsT = sb("wsT", [128, 8, 128], BF16)
        ws8 = sb("ws8", [128, 8, 128], BF16)
        poolw = sb("poolw", [128, 4, 256], BF16)
        modT = sb("modT", [128, 72, 17], F32)
        At = sb("At", [128, 3, 8, 17], F32)
        Gt = sb("Gt", [128, 3, 8, 17], F32)
        qhist = sb("qhist", [128, 4, 2], F32)
        phist = sb("phist", [128, 4, 15], F32)
        qs = sb("qs", [128, 4, 16, 10], F32)
        pss = sb("pss", [128, 4, 16, 23], F32)
        small = sb("small", [128, 64], F32)
        epsb = sb("epsb", [128, 1], F32)
        mstg = sb("mstg", [32, 512], F32)
        psb = [es.enter_context(nc.psum_tensor(f"psb{i}", [128, 512], F32)) for i in range(8)]

        sem_names = ENGS + [f"ring{i}" for i in range(NSLOT)] + ["dn", "cst", "xin", "lay", "hist", "out", "stg0", "stg1", "nvs", "bt", "layp"]
        sems = {n: es.enter_context(nc.semaphore(n)) for n in sem_names}

        PS = Ring(psb[0:7])
        modB = psb[7]
        RG = Ring(ring_t)

        def av(off, shape, dt=BF16):
            n = 1
            for s in shape[1:]:
                n *= s
            if dt == F32:
                assert off % 2 == 0
                v = arena[:, off:off + 2 * n].bitcast(F32)
            else:
                v = arena[:, off:off + n]
            if len(shape) == 2:
                return v
            if len(shape) == 3:
                return v.rearrange("p (a b) -> p a b", a=shape[1])
            if len(shape) == 4:
                return v.rearrange("p (a b c) -> p a b c", a=shape[1], b=shape[2])
            raise ValueError

        H_OFF = 0
        G_OFF = 17408
        DN_OFF = 34816
        FT_OFF = 43008
        MH_OFF = 0
        ZA_OFF = 6144; ZB_OFF = 9216; ZC_OFF = 12288
        MG_OFF = 15360
        VN_OFF = 21504
        TMP_OFF = 24576
        hfull = av(H_OFF, [128, 8, T])
        gbuf = av(G_OFF, [128, 8, T])
        dnbuf = av(DN_OFF, [128, 8, 1024])
        silu_t = [av(FT_OFF + 1024 * i, [128, 512], F32) for i in range(2)]
        gate_t = av(FT_OFF + 2048, [128, 128], F32)
        hpart = av(MH_OFF, [128, 8, 768])
        zA = av(ZA_OFF, [128, 4, 768]); zB = av(ZB_OFF, [128, 4, 768]); zC = av(ZC_OFF, [128, 4, 768])
        merged = av(MG_OFF, [128, 8, 768])
        vn = av(VN_OFF, [128, 6, 512])

        cst = []
        cst.append(P.dma('sp', lambda h: h.dma_start(out=ident[:], in_=ident_d), 'cst'))
        cst.append(P.dma('sp', lambda h: h.dma_start(out=masks[:], in_=masks_d), 'cst'))
        cst.append(P.dma('sp', lambda h: h.dma_start(out=invcnt[:], in_=invcnt_d), 'cst'))
        cst.append(P.dma('sp', lambda h: h.dma_start(out=fin[:], in_=fin_d), 'cst'))
        cTf = av(G_OFF + 8192, [128, 8, 17], F32)
        cst.append(P.dma('sp', lambda h: h.dma_start(out=cTf, in_=cT_d), 'cst'))
        t_cst = cst[-1]
        t_ones = P.op('dve', lambda h: h.memset(ones[:], 1.0))
        t_eps = P.op('dve', lambda h: h.memset(epsb[:], EPS))
        t_sc = P.op('act', lambda h: h.activation(out=scT[:], in_=cTf, func=AF.Silu), [t_cst])

        def wslab(srcs):
            i, slot, deps = RG.acquire()
            tok = None
            for ent in srcs:
                if len(ent) == 3:
                    src, c0, n = ent
                    k0, k1 = 0, 8
                else:
                    src, c0, n, k0, k1 = ent
                tok = P.dma('pool', (lambda h, slot=slot, src=src, c0=c0, n=n, k0=k0, k1=k1: h.dma_start(out=slot[:, k0:k1, c0:c0 + n], in_=src)),
                            f"ring{i}", deps)
            return i, slot, tok

        def wcols(w2d, col, n=128):
            return w2d.rearrange("(k p) n -> p k n", p=128)[:, :, col:col + n]

        def evac_bank():
            return PS.acquire()

        stage = [av(G_OFF + 2048 * i, [128, 1024], F32) for i in range(2)]
        stage_rel = [[], []]
        xw_tok = {}
        for ti in range(17):
            b = ti % 2
            src = xp_d[ti * 128:(ti + 1) * 128, :] if ti < 16 else xs_d
            t_ld = P.dma('sp', (lambda h, b=b, src=src: h.dma_start(out=stage[b], in_=src)), f"stg{b}", stage_rel[b])
            toks = []
            last_tr = None
            for a in range(2):
                bi, bank, bdeps = PS.acquire()
                tr = None
                for j in range(4):
                    dc = a * 4 + j
                    tr = P.op('pe', (lambda h, bank=bank, b=b, dc=dc, j=j: h.transpose(
                        out=bank[:, j * 128:(j + 1) * 128], in_=stage[b][:, dc * 128:(dc + 1) * 128], identity=ident[:])),
                        [t_ld, t_cst] + bdeps)
                last_tr = tr
                eng = 'act' if a == 0 else 'dve'
                if eng == 'act':
                    tc = P.op('act', (lambda h, bank=bank, a=a, ti=ti: h.activation(
                        out=xT[:, a * 4:(a + 1) * 4, ti * 128:(ti + 1) * 128],
                        in_=bank[:].rearrange("p (a b) -> p a b", a=4), func=AF.Copy)), [tr])
                else:
                    tc = P.op('dve', (lambda h, bank=bank, a=a, ti=ti: h.tensor_copy(
                        out=xT[:, a * 4:(a + 1) * 4, ti * 128:(ti + 1) * 128],
                        in_=bank[:].rearrange("p (a b) -> p a b", a=4))), [tr])
                PS.release(bi, [tc])
                toks.append(tc)
            stage_rel[b] = [last_tr]
            xw_tok[ti] = toks

        def xdeps(c0, n):
            r = []
            for ti in range(c0 // 128, (c0 + n + 127) // 128):
                r += xw_tok.get(ti, [])
            return r

        def set_xw(c0, n, toks):
            for ti in range(c0 // 128, (c0 + n + 127) // 128):
                xw_tok[ti] = list(toks)

        lay_tok = {}

        def load_layer_small(l, d=None):
            if d is None:
                d = P.now()
            t = []
            t.append(P.dma('sp', lambda h: h.dma_start(out=vecs[:], in_=vecs_d[l]), 'lay', d))
            t.append(P.dma('sp', lambda h: h.dma_start(out=lng[:], in_=lng_d[l:l + 1, :].partition_broadcast(128)), 'lay', d))
            t.append(P.dma('sp', lambda h: h.dma_start(out=biasp[:], in_=biasp_d[l]), 'lay', d))
            t.append(P.dma('sp', lambda h: h.dma_start(out=biass[:], in_=biass_d[l]), 'lay', d))
            t_sp = t[-1]
            t1 = P.dma('pool', lambda h: h.dma_start(out=wsT[:], in_=wsT_d[l]), 'layp', d)
            t2 = P.dma('pool', lambda h: h.dma_start(out=ws8[:], in_=ws8_d[l]), 'layp', d)
            t3 = P.dma('pool', lambda h: h.dma_start(out=poolw[:], in_=poolw_d[l].rearrange("g c d -> c g d")), 'layp', d)
            m1 = P.op('pool', lambda h: h.tensor_tensor(out=wsT[:], in0=wsT[:], in1=masks[:, 0:1, :].broadcast_to([128, 8, 128]), op=ALU.mult), [t3, t_sp, t_cst])
            m2 = P.op('pool', lambda h: h.tensor_tensor(out=ws8[:], in0=ws8[:], in1=masks[:, 1:2, :].broadcast_to([128, 8, 128]), op=ALU.mult), [m1])
            lay_tok[l] = m2
            return m2

        class ModJob:
            def __init__(self, l):
                self.l = l
                self.ct = 0
                self.toks = []

            def step(self):
                self._flush()
                if self.ct >= 18:
                    return
                l = self.l
                ct = self.ct
                self.ct += 1
                bi, bank, bdeps = PS.acquire()
                last = None
                for hf in range(2):
                    si, slot, wt = wslab([(wcols(w_ada_d[l], ct * 512 + hf * 256, 256), 0, 256)])
                    for k in range(8):
                        last = P.op('pe', (lambda h, bank=bank, slot=slot, hf=hf, k=k: h.matmul(
                            bank[0:17, hf * 256:(hf + 1) * 256], lhsT=scT[:, k, :], rhs=slot[:, k, :],
                            start=(k == 0), stop=(k == 7))), [wt, t_sc] + bdeps)
                    RG.release(si, [last])
                tcp = P.op('act', (lambda h, bank=bank: h.activation(out=mstg[0:17, :], in_=bank[0:17, :], func=AF.Copy)), [last] + modst['stg_rel'])
                PS.release(bi, [tcp])
                self.pending = (ct, tcp)

            def _flush(self):
                if getattr(self, 'pending', None) is None:
                    return
                ct, tcp = self.pending
                self.pending = None
                l = self.l
                tr = None
                for j in range(4):
                    cc = (ct % 6) * 4 + j
                    tr = P.op('pe', (lambda h, j=j, cc=cc: h.transpose(
                        out=modB[:, cc * 17:(cc + 1) * 17], in_=mstg[0:17, j * 128:(j + 1) * 128], identity=ident[0:17, 0:17])),
                        [tcp, t_cst] + modst['B_rel'])
                modst['stg_rel'] = [tr]
                if ct % 6 == 5:
                    bnk = ct // 6
                    t = P.op('dve', (lambda h, bnk=bnk: h.tensor_tensor(
                        out=modT[:, bnk * 24:(bnk + 1) * 24, :], in0=modB[:, 0:408].rearrange("p (a b) -> p a b", a=24),
                        in1=vecs[:, bnk * 24:(bnk + 1) * 24].unsqueeze(2).broadcast_to([128, 24, 17]), op=ALU.add)),
                        [tr, lay_tok[l]] + getattr(self, 'extra', []))
                    modst['B_rel'] = [t]
                    self.toks.append(t)

            def finish(self):
                while self.ct < 18:
                    self.step()
                self._flush()
                return self.toks

        modst = {'stg_rel': [], 'B_rel': []}

        def finish_mod(l, toks, subs=(0, 1, 2)):
            out = []
            for sub in subs:
                ta = P.op('dve', (lambda h, sub=sub: h.scalar_tensor_tensor(
                    out=At[:, sub, :, :], in0=modT[:, sub * 24 + 8:sub * 24 + 16, :], scalar=1.0,
                    in1=vecs[:, 72 + sub * 8:72 + sub * 8 + 8].unsqueeze(2).broadcast_to([128, 8, 17]),
                    op0=ALU.add, op1=ALU.mult)), toks)
                tg = P.op('dve', (lambda h, sub=sub: h.tensor_scalar(
                    out=Gt[:, sub, :, :], in0=modT[:, sub * 24 + 16:sub * 24 + 24, :], scalar1=(1.0 if sub == 1 else 0.5), scalar2=None,
                    op0=ALU.mult)), toks)
                out += [ta, tg]
            return out

        def Bsh(sub, k, s0, s1):
            return modT[:, sub * 24 + k, s0:s1]

        def norm(sub, tts, hdst, hcol0, noff, mod_toks, final=False, extra=()):
            sqh = [av(noff + 2048 * i, [128, 4, 512]) for i in range(2)]
            tmpk = [av(noff + 4096 + 1024 * i, [128, 512], F32) for i in range(4)]
            tmps = av(noff + 4096, [128, 8, 128], F32)
            tmp2 = [av(noff + 4096 + 2048 * i, [128, 2, 512], F32) for i in range(2)]
            htoks = {}
            sq_rel = [[], []]
            held = {}
            for (c0, n) in tts:
                bi, bank, bdeps = PS.acquire()
                mm = None
                for hf in range(2):
                    t_sq = P.op('act', (lambda h, c0=c0, n=n, hf=hf: h.activation(out=sqh[hf][:, :, 0:n], in_=xT[:, hf * 4:hf * 4 + 4, c0:c0 + n], func=AF.Square)),
                                xdeps(c0, n) + sq_rel[hf] + list(extra))
                    for k in range(4):
                        mm = P.op('pe', (lambda h, bank=bank, k=k, n=n, hf=hf: h.matmul(bank[:, 0:n], lhsT=ones[:], rhs=sqh[hf][:, k, 0:n],
                                                                                        start=(hf == 0 and k == 0), stop=(hf == 1 and k == 3))), [t_sq, t_ones] + bdeps)
                    sq_rel[hf] = [mm]
                t_rt = P.op('act', (lambda h, bank=bank, n=n: h.activation(out=bank[:, 0:n], in_=bank[:, 0:n], func=AF.Sqrt,
                                                                         scale=1.0 / D, bias=epsb[:, 0:1])), [mm, t_eps])
                t_rs = P.op('dve', (lambda h, bank=bank, n=n: h.reciprocal(out=bank[:, 0:n], in_=bank[:, 0:n])), [t_rt])
                held[(c0, n)] = (bi, bank, t_rs)
            tk_rel = [[], [], [], []]
            kk = 0
            for (c0, n) in tts:
                bi, bank, t_rs = held[(c0, n)]
                toks = []
                if final:
                    for k in range(8):
                        t = P.op('dve', (lambda h, bank=bank, k=k, c0=c0, n=n: h.scalar_tensor_tensor(
                            out=xT[:, k, c0:c0 + n], in0=xT[:, k, c0:c0 + n], scalar=fin[:, k:k + 1], in1=bank[:, 0:n],
                            op0=ALU.mult, op1=ALU.mult)), [t_rs])
                        toks.append(t)
                    set_xw(c0, n, toks)
                elif c0 < TP:
                    for kp in range(4):
                        b = kk % 2
                        kk += 1
                        t1 = P.op('dve', (lambda h, bank=bank, kp=kp, c0=c0, n=n, b=b: h.tensor_tensor(
                            out=tmp2[b][:, :, 0:n], in0=xT[:, 2 * kp:2 * kp + 2, c0:c0 + n],
                            in1=bank[:, 0:n].unsqueeze(1).broadcast_to([128, 2, n]), op=ALU.mult)), [t_rs] + tk_rel[b])
                        for j in range(2):
                            k = 2 * kp + j
                            t2 = P.op('act', (lambda h, k=k, j=j, c0=c0, n=n, b=b: h.activation(
                                out=hdst[:, k, c0 - hcol0:c0 - hcol0 + n], in_=tmp2[b][:, j, 0:n], func=AF.Identity,
                                scale=At[:, sub, k, 0:1], bias=Bsh(sub, k, 0, 1))), [t1] + mod_toks)
                            toks.append(t2)
                        tk_rel[b] = [t2]
                    last_t1 = t1
                else:
                    t1 = P.op('dve', (lambda h, bank=bank, c0=c0, n=n: h.tensor_tensor(
                        out=tmps[:], in0=xT[:, :, c0:c0 + n], in1=bank[:, 0:n].unsqueeze(1).broadcast_to([128, 8, n]), op=ALU.mult)),
                        [t_rs] + tk_rel[0] + tk_rel[1])
                    t2 = P.op('dve', (lambda h: h.tensor_tensor(
                        out=tmps[:].rearrange("p k (s t) -> p k s t", t=8), in0=tmps[:].rearrange("p k (s t) -> p k s t", t=8),
                        in1=At[:, sub, :, 1:17].unsqueeze(3).broadcast_to([128, 8, 16, 8]), op=ALU.mult)), [t1] + mod_toks)
                    t3 = P.op('dve', (lambda h, c0=c0, n=n: h.tensor_tensor(
                        out=hdst[:, :, c0 - hcol0:c0 - hcol0 + n].rearrange("p k (s t) -> p k s t", t=8),
                        in0=tmps[:].rearrange("p k (s t) -> p k s t", t=8),
                        in1=modT[:, sub * 24:sub * 24 + 8, 1:17].unsqueeze(3).broadcast_to([128, 8, 16, 8]), op=ALU.add)), [t2])
                    tk_rel[0] = [t3]
                    tk_rel[1] = [t3]
                    toks.append(t3)
                PS.release(bi, toks if final else ([last_t1] if c0 < TP else toks[-1:]))
                htoks[(c0, n)] = toks
            return htoks

        gate_rel = []
        def resid_update(bank, i, c0, n, sub, mm, extra=()):
            if c0 < TP:
                t = P.op('dve', (lambda h, bank=bank, i=i, c0=c0, n=n: h.scalar_tensor_tensor(
                    out=xT[:, i, c0:c0 + n], in0=bank[:, 0:n], scalar=Gt[:, sub, i, 0:1], in1=xT[:, i, c0:c0 + n],
                    op0=ALU.mult, op1=ALU.add)), [mm] + list(extra))
                return t
            t1 = P.op('dve', (lambda h, bank=bank, i=i, n=n: h.tensor_tensor(
                out=gate_t[:, 0:n].rearrange("p (s t) -> p s t", t=8), in0=bank[:, 0:n].rearrange("p (s t) -> p s t", t=8),
                in1=Gt[:, sub, i, 1:17].unsqueeze(2).broadcast_to([128, 16, 8]), op=ALU.mult)), [mm] + list(extra) + gate_rel)
            t2 = P.op('dve', (lambda h, i=i, c0=c0, n=n: h.tensor_tensor(
                out=xT[:, i, c0:c0 + n], in0=xT[:, i, c0:c0 + n], in1=gate_t[:, 0:n], op=ALU.add)), [t1])
            gate_rel[:] = [t2]
            return t2

        def ffn(l, wgu_d, wdn_d, sub, htoks, hook=None):
            wgu = wgu_d[l]
            wdn = wdn_d[l]
            dn_rel = list(outs[-1:]) + [t for v in htoks.values() for t in v]
            g_readers = []
            sl_rel = [[], []]
            sidx = 0
            xlast = {}
            for grp in FF_GROUPS:
                gtok = {}
                for jj, j in enumerate(grp):
                    si, slot, wt = wslab([(wcols(wgu, j * 128), 0, 128), (wcols(wgu, DFF + j * 128), 128, 128)])
                    mm = None
                    for (c0, n) in TT_ALL:
                        ba, banka, da = PS.acquire()
                        bb, bankb, db = PS.acquire()
                        hd = htoks[(c0, n)]
                        for k in range(8):
                            mma = P.op('pe', (lambda h, banka=banka, slot=slot, k=k, c0=c0, n=n: h.matmul(
                                banka[:, 0:n], lhsT=slot[:, k, 0:128], rhs=hfull[:, k, c0:c0 + n], start=(k == 0), stop=(k == 7))),
                                [wt] + hd + da)
                        for k in range(8):
                            mm = P.op('pe', (lambda h, bankb=bankb, slot=slot, k=k, c0=c0, n=n: h.matmul(
                                bankb[:, 0:n], lhsT=slot[:, k, 128:256], rhs=hfull[:, k, c0:c0 + n], start=(k == 0), stop=(k == 7))),
                                [wt] + hd + db)
                        sb_ = sidx % 2
                        sidx += 1
                        ta = P.op('act', (lambda h, banka=banka, n=n, sb_=sb_: h.activation(out=silu_t[sb_][:, 0:n], in_=banka[:, 0:n], func=AF.Silu)),
                                  [mma] + sl_rel[sb_])
                        PS.release(ba, [ta])
                        tv = P.op('dve', (lambda h, bankb=bankb, jj=jj, c0=c0, n=n, sb_=sb_: h.tensor_tensor(
                            out=gbuf[:, jj, c0:c0 + n], in0=bankb[:, 0:n], in1=silu_t[sb_][:, 0:n], op=ALU.mult)), [mm, ta] + g_readers)
                        sl_rel[sb_] = [tv]
                        PS.release(bb, [tv])
                        gtok.setdefault((c0, n), []).append(tv)
                    RG.release(si, [mm])
                    if hook is not None:
                        hook()
                g_readers = []
                G = len(grp)
                j0 = grp[0]
                t_dn = P.dma('pool', (lambda h, G=G, j0=j0: h.dma_start(
                    out=dnbuf[:, 0:G, :], in_=wdn[j0 * 128:(j0 + G) * 128, :].rearrange("(j p) n -> p j n", p=128))), 'dn', dn_rel)
                mmd = None
                for i in range(8):
                    for (c0, n) in TT_ALL:
                        bi, bank, bdeps = PS.acquire()
                        for jj in range(G):
                            mmd = P.op('pe', (lambda h, bank=bank, jj=jj, i=i, c0=c0, n=n, G=G: h.matmul(
                                bank[:, 0:n], lhsT=dnbuf[:, jj, i * 128:(i + 1) * 128], rhs=gbuf[:, jj, c0:c0 + n],
                                start=(jj == 0), stop=(jj == G - 1))), [t_dn] + gtok[(c0, n)] + bdeps)
                        t = resid_update(bank, i, c0, n, sub, mmd, xlast.get((i, c0), []))
                        xlast[(i, c0)] = [t]
                        PS.release(bi, [t])
                dn_rel = [mmd]
                g_readers = [mmd]
            for (c0, n) in TT_ALL:
                set_xw(c0, n, [xlast[(i, c0)][0] for i in range(8)])

        def cwf(j, c):
            return vecs[:, 96 + j * 4 + c:96 + j * 4 + c + 1]

        def mixer_part(l, pi, tts, mod_toks, pre=None, next_tts=None):
            pc0 = tts[0][0]
            Th = sum(n for _, n in tts)
            win = w_in_d[l]
            has_sample = any(c0 >= TP for c0, _ in tts)
            ptts = [(c0, n) for (c0, n) in tts if c0 < TP]
            Tpp = sum(n for _, n in ptts)
            first = (pi == 0)
            if pre is None:
                P.barrier()
                htoks = norm(1, tts, hpart, pc0, DN_OFF, mod_toks)
                P.floor = []
            else:
                htoks = pre
            hall = [t for v in htoks.values() for t in v]

            import os
            MSTOP = int(os.environ.get("KMSTOP", "99"))
            if MSTOP <= 1:
                return
            qb = [av(TMP_OFF + 1544 * i, [128, 772], F32) for i in range(2)]
            xa_sb = [av(TMP_OFF + 3088 + 1024 * i, [128, 512], F32) for i in range(2)]
            cvt = [av(TMP_OFF + 5136 + 1536 * i, [128, 768], F32) for i in range(2)]
            xa_rel = [[], []]
            xi = 0
            hist_t = None
            if has_sample:
                hist_t = P.dma('sp', lambda h: h.dma_start(out=qs[:], in_=sconvT_d[l]), 'hist', P.now())
                hist_t = P.dma('sp', lambda h: h.dma_start(out=pss[:], in_=spoolT_d[l]), 'hist', P.now())
            za_toks = []
            qb_rel = [[], []]
            for c in range(4):
                q = qb[c % 2]
                cv = cvt[c % 2]
                if first:
                    th = P.op('dve', (lambda h, q=q: h.memset(q[:, 0:2], 0.0)), qb_rel[c % 2])
                else:
                    th = P.op('dve', (lambda h, q=q, c=c: h.tensor_copy(out=q[:, 0:2], in_=qhist[:, c, :])), qb_rel[c % 2])
                si, slot, wt = wslab([(wcols(win, c * 128), 0, 128), (wcols(win, 1024 + c * 128), 128, 128)])
                qtoks = []
                mm = None
                for (c0, n) in tts:
                    lc = c0 - pc0
                    ba, banka, da = PS.acquire()
                    bb, bankb, db = PS.acquire()
                    for k in range(8):
                        mma = P.op('pe', (lambda h, banka=banka, slot=slot, k=k, lc=lc, n=n: h.matmul(
                            banka[:, 0:n], lhsT=slot[:, k, 0:128], rhs=hpart[:, k, lc:lc + n], start=(k == 0), stop=(k == 7))), [wt] + hall + da)
                    for k in range(8):
                        mm = P.op('pe', (lambda h, bankb=bankb, slot=slot, k=k, lc=lc, n=n: h.matmul(
                            bankb[:, 0:n], lhsT=slot[:, k, 128:256], rhs=hpart[:, k, lc:lc + n], start=(k == 0), stop=(k == 7))), [wt] + hall + db)
                    xb_ = xi % 2
                    xi += 1
                    ta = P.op('act', (lambda h, banka=banka, n=n, xb_=xb_: h.activation(out=xa_sb[xb_][:, 0:n], in_=banka[:, 0:n], func=AF.Copy)),
                              [mma] + xa_rel[xb_])
                    PS.release(ba, [ta])
                    if c0 < TP:
                        tq = P.op('dve', (lambda h, bankb=bankb, q=q, lc=lc, n=n, xb_=xb_: h.tensor_tensor(
                            out=q[:, 2 + lc:2 + lc + n], in0=bankb[:, 0:n], in1=xa_sb[xb_][:, 0:n], op=ALU.mult)), [mm, ta, th])
                    else:
                        tq = P.op('dve', (lambda h, bankb=bankb, c=c, n=n, xb_=xb_: h.tensor_tensor(
                            out=qs[:, c, :, 2:10], in0=bankb[:, 0:n].rearrange("p (s t) -> p s t", t=8),
                            in1=xa_sb[xb_][:, 0:n].rearrange("p (s t) -> p s t", t=8), op=ALU.mult)), [mm, ta, hist_t])
                    xa_rel[xb_] = [tq]
                    PS.release(bb, [tq])
                    qtoks.append(tq)
                RG.release(si, [mm])
                ctoks = []
                if Tpp > 0:
                    t1 = P.op('dve', (lambda h, q=q, cv=cv, c=c: h.tensor_scalar(out=cv[:, 0:Tpp], in0=q[:, 2:2 + Tpp], scalar1=cwf(2, c), scalar2=None, op0=ALU.mult)),
                              qtoks + [lay_tok[l]])
                    t2 = P.op('dve', (lambda h, q=q, cv=cv, c=c: h.scalar_tensor_tensor(out=cv[:, 0:Tpp], in0=q[:, 1:1 + Tpp], scalar=cwf(1, c), in1=cv[:, 0:Tpp],
                                                                                     op0=ALU.mult, op1=ALU.add)), [t1])
                    t3 = P.op('dve', (lambda h, q=q, cv=cv, c=c: h.scalar_tensor_tensor(out=cv[:, 0:Tpp], in0=q[:, 0:Tpp], scalar=cwf(0, c), in1=cv[:, 0:Tpp],
                                                                                     op0=ALU.mult, op1=ALU.add)), [t2])
                    t4 = P.op('dve', (lambda h, q=q, c=c: h.tensor_copy(out=qhist[:, c, :], in_=q[:, Tpp:Tpp + 2])), [t3])
                    ctoks = [t3, t4]
                if has_sample:
                    cvs = cv[:, Tpp:Tpp + 128].rearrange("p (s t) -> p s t", t=8)
                    t1 = P.op('dve', (lambda h, cvs=cvs, c=c: h.tensor_scalar(out=cvs, in0=qs[:, c, :, 2:10], scalar1=cwf(2, c), scalar2=None, op0=ALU.mult)),
                              qtoks + [lay_tok[l]])
                    t2 = P.op('dve', (lambda h, cvs=cvs, c=c: h.scalar_tensor_tensor(out=cvs, in0=qs[:, c, :, 1:9], scalar=cwf(1, c), in1=cvs, op0=ALU.mult, op1=ALU.add)), [t1])
                    t3 = P.op('dve', (lambda h, cvs=cvs, c=c: h.scalar_tensor_tensor(out=cvs, in0=qs[:, c, :, 0:8], scalar=cwf(0, c), in1=cvs, op0=ALU.mult, op1=ALU.add)), [t2])
                    ctoks.append(t3)
                si, slot, wt = wslab([(wcols(win, 512 + c * 128), 0, 128)])
                mm = None
                last_z = []
                for (c0, n) in tts:
                    lc = c0 - pc0
                    bi, bank, bd = PS.acquire()
                    for k in range(8):
                        mm = P.op('pe', (lambda h, bank=bank, slot=slot, k=k, lc=lc, n=n: h.matmul(
                            bank[:, 0:n], lhsT=slot[:, k, 0:128], rhs=hpart[:, k, lc:lc + n], start=(k == 0), stop=(k == 7))), [wt] + hall + bd)
                    tz = P.op('dve', (lambda h, bank=bank, cv=cv, c=c, lc=lc, n=n: h.tensor_tensor(
                        out=zA[:, c, lc:lc + n], in0=bank[:, 0:n], in1=cv[:, lc:lc + n], op=ALU.mult)), [mm] + ctoks)
                    PS.release(bi, [tz])
                    last_z.append(tz)
                RG.release(si, [mm])
                qb_rel[c % 2] = last_z + ctoks
                za_toks += last_z
            if MSTOP <= 2:
                return

            ntile = Th // 128
            vg = av(DN_OFF, [128, 6, 512], F32)
            ub = [av(TMP_OFF + 8208 + 768 * i, [128, 768]) for i in range(2)]
            sgt = [av(DN_OFF + 6144 + 1024 * i, [128, 512], F32) for i in range(2)]
            stats = small[:, 0:36].rearrange("p (a b) -> p a b", a=6)
            mv = small[:, 36:48].rearrange("p (a b) -> p a b", a=6)
            rsd = small[:, 48:54]
            s1i, slot1, wt1 = wslab([(wcols(win, 2048, 256), 0, 256)])
            s2i, slot2, wt2 = wslab([(wcols(win, 2304, 256), 0, 256)])
            mm = None
            st_toks = []
            for ti in range(ntile):
                lt = ti * 128
                bi, bank, bd = PS.acquire()
                for hf, (slot, wt) in enumerate([(slot1, wt1), (slot2, wt2)]):
                    for k in range(8):
                        mm = P.op('pe', (lambda h, bank=bank, slot=slot, k=k, lt=lt, hf=hf: h.matmul(
                            bank[:, hf * 256:(hf + 1) * 256], lhsT=hpart[:, k, lt:lt + 128], rhs=slot[:, k, :], start=(k == 0), stop=(k == 7))),
                            [wt] + hall + bd)
                tg = P.op('act', (lambda h, bank=bank, ti=ti: h.activation(out=vg[:, ti, :], in_=bank[:], func=AF.Gelu)), [mm])
                PS.release(bi, [tg])
                ts = P.op('dve', (lambda h, ti=ti: h.bn_stats(out=stats[:, ti, :], in_=vg[:, ti, :])), [tg])
                ta = P.op('dve', (lambda h, ti=ti: h.bn_aggr(out=mv[:, ti, :], in_=stats[:, ti, :])), [ts])
                st_toks.append(ta)
            RG.release(s1i, [mm])
            RG.release(s2i, [mm])
            ub_rel = [[], []]

            def u_proj(c):
                u = ub[c % 2]
                si, slot, wt = wslab([(wcols(win, 1536 + c * 128), 0, 128)])
                utoks = []
                mmu = None
                for (c0, n) in tts:
                    lc = c0 - pc0
                    bi, bank, bd = PS.acquire()
                    for k in range(8):
                        mmu = P.op('pe', (lambda h, bank=bank, slot=slot, k=k, lc=lc, n=n: h.matmul(
                            bank[:, 0:n], lhsT=slot[:, k, 0:128], rhs=hpart[:, k, lc:lc + n], start=(k == 0), stop=(k == 7))), [wt] + hall + bd)
                    tu = P.op('act', (lambda h, bank=bank, u=u, lc=lc, n=n: h.activation(out=u[:, lc:lc + n], in_=bank[:, 0:n], func=AF.Gelu)), [mmu] + ub_rel[c % 2])
                    PS.release(bi, [tu])
                    utoks.append(tu)
                RG.release(si, [mmu])
                return utoks

            W_ = 15 + 768 + 1
            pb_ = [av(TMP_OFF + 1568 * i, [128, W_], F32) for i in range(3)]
            pt_ = [av(TMP_OFF + 4704 + 1568 * i, [128, W_], F32) for i in range(2)]
            fx = av(TMP_OFF + 7840, [128, 16], F32)
            pst = av(TMP_OFF + 7872, [128, 2, 16, 23], F32)
            pb_rel = [[], [], []]
            ppre = {}

            def p_proj(c):
                pbuf = pb_[c % 3]
                if first:
                    th = P.op('dve', (lambda h, pbuf=pbuf: h.memset(pbuf[:, 0:15], 0.0)), pb_rel[c % 3] + za_toks)
                else:
                    th = P.op('dve', (lambda h, pbuf=pbuf, c=c: h.tensor_copy(out=pbuf[:, 0:15], in_=phist[:, c, :])), pb_rel[c % 3] + za_toks)
                si, slot, wt = wslab([(wcols(win, 2560 + c * 128), 0, 128)])
                ptoks = []
                mmp = None
                for (c0, n) in tts:
                    lc = c0 - pc0
                    bi, bank, bd = PS.acquire()
                    for k in range(8):
                        mmp = P.op('pe', (lambda h, bank=bank, slot=slot, k=k, lc=lc, n=n: h.matmul(
                            bank[:, 0:n], lhsT=slot[:, k, 0:128], rhs=hpart[:, k, lc:lc + n], start=(k == 0), stop=(k == 7))), [wt] + hall + bd)
                    if c0 < TP:
                        tp = P.op('act', (lambda h, bank=bank, pbuf=pbuf, lc=lc, n=n: h.activation(out=pbuf[:, 15 + lc:15 + lc + n], in_=bank[:, 0:n], func=AF.Copy)), [mmp, th])
                    else:
                        tp = P.op('act', (lambda h, bank=bank, c=c, n=n: h.activation(out=pss[:, c, :, 15:23], in_=bank[:, 0:n].rearrange("p (s t) -> p s t", t=8), func=AF.Copy)),
                                  [mmp, hist_t])
                    PS.release(bi, [tp])
                    ptoks.append(tp)
                RG.release(si, [mmp])
                return pbuf, ptoks

            upre = {0: u_proj(0), 1: u_proj(1)}
            for c_ in range(3):
                ppre[c_] = p_proj(c_)
            t_sq = P.op('act', (lambda h: h.activation(out=rsd[:, 0:ntile], in_=mv[:, 0:ntile, 1], func=AF.Sqrt, scale=1.0, bias=epsb[:, 0:1])), st_toks + [t_eps])
            t_rc = P.op('dve', (lambda h: h.reciprocal(out=rsd[:, 0:ntile], in_=rsd[:, 0:ntile])), [t_sq])
            vn_toks = []
            for ti in range(ntile):
                t1 = P.op('dve', (lambda h, ti=ti: h.tensor_scalar(out=vg[:, ti, :], in0=vg[:, ti, :], scalar1=mv[:, ti, 0:1], scalar2=rsd[:, ti:ti + 1],
                                                                 op0=ALU.subtract, op1=ALU.mult)), [t_rc])
                is_s = has_sample and ti == ntile - 1
                if is_s:
                    t2 = P.op('dve', (lambda h, ti=ti: h.tensor_tensor(out=vg[:, ti, :], in0=vg[:, ti, :], in1=lng[:], op=ALU.mult)), [t1, lay_tok[l]])
                    t3 = P.op('dve', (lambda h, ti=ti: h.tensor_copy(out=vn[:, ti, :], in_=vg[:, ti, :])), [t2])
                    nvs_tok = [P.dma('sp', (lambda h, ti=ti: h.dma_start(out=nvs_d[l], in_=vg[:, ti, :])), 'nvs', [t2])]
                    fin_extra.append(nvs_tok[0])
                    vn_toks.append(t3)
                else:
                    t2 = P.op('dve', (lambda h, ti=ti: h.tensor_tensor(out=vn[:, ti, :], in0=vg[:, ti, :], in1=lng[:], op=ALU.mult)), [t1, lay_tok[l]])
                    vn_toks.append(t2)
            sg_rel = [[], []]
            sgi = 0
            zb_toks = []
            sg_last = None
            for c in range(4):
                u = ub[c % 2]
                utoks = upre[c]
                ztk = []
                for (c0, n) in tts:
                    lc = c0 - pc0
                    bi, bank, bd = PS.acquire()
                    mm = None
                    for tj in range(n // 128):
                        ti = (lc + tj * 128) // 128
                        is_s = (c0 >= TP)
                        wsrc = ws8 if is_s else wsT
                        for hh in range(2):
                            hd = 2 * c + hh
                            mm = P.op('pe', (lambda h, bank=bank, ti=ti, tj=tj, hh=hh, hd=hd, wsrc=wsrc: h.matmul(
                                bank[hh * 64:(hh + 1) * 64, tj * 128:(tj + 1) * 128], lhsT=vn[:, ti, hd * 64:(hd + 1) * 64],
                                rhs=wsrc[:, hd, :], start=True, stop=True)), vn_toks + [lay_tok[l]] + bd)
                    sg_last = mm
                    sb_ = sgi % 2
                    sgi += 1
                    bsrc = biass if c0 >= TP else biasp
                    t1 = P.op('dve', (lambda h, bank=bank, n=n, sb_=sb_, bsrc=bsrc, c=c: h.tensor_tensor(
                        out=sgt[sb_][:, 0:n].rearrange("p (a b) -> p a b", b=128), in0=bank[:, 0:n].rearrange("p (a b) -> p a b", b=128),
                        in1=bsrc[:, c:c + 1, :].broadcast_to([128, n // 128, 128]), op=ALU.add)), [mm, lay_tok[l]] + sg_rel[sb_])
                    PS.release(bi, [t1])
                    t2 = P.op('dve', (lambda h, u=u, c=c, lc=lc, n=n, sb_=sb_: h.tensor_tensor(
                        out=zB[:, c, lc:lc + n], in0=sgt[sb_][:, 0:n], in1=u[:, lc:lc + n], op=ALU.mult)), [t1] + utoks)
                    sg_rel[sb_] = [t2]
                    ztk.append(t2)
                ub_rel[c % 2] = ztk
                if c + 2 < 4:
                    upre[c + 2] = u_proj(c + 2)
                zb_toks += ztk
            if MSTOP <= 3:
                return

            zc_toks = []
            for c in range(4):
                w = WINS[c]
                if c not in ppre:
                    ppre[c] = p_proj(c)
                pbuf, ptoks = ppre[c]
                ztk = []
                if Tpp > 0:
                    L = 15 + Tpp
                    src = pbuf
                    sh = 1
                    tprev = ptoks
                    lvl = 0
                    while sh < w:
                        dst = pt_[lvl % 2]
                        tprev = [P.op('dve', (lambda h, src=src, dst=dst, sh=sh, L=L: h.tensor_tensor(
                            out=dst[:, sh:L], in0=src[:, sh:L], in1=src[:, 0:L - sh], op=ALU.add)), tprev)]
                        src = dst
                        sh *= 2
                        lvl += 1
                    tz = P.op('dve', (lambda h, src=src, pbuf=pbuf, c=c, w=w: h.scalar_tensor_tensor(
                        out=zC[:, c, 0:Tpp], in0=src[:, 15:15 + Tpp], scalar=1.0 / w, in1=pbuf[:, 15:15 + Tpp], op0=ALU.mult, op1=ALU.subtract)), tprev)
                    ztk.append(tz)
                    if first:
                        tf1 = P.op('dve', (lambda h, src=src, c=c, w=w: h.tensor_tensor(out=fx[:, 0:w - 1], in0=src[:, 15:15 + w - 1], in1=invcnt[:, c, 0:w - 1], op=ALU.mult)),
                                   [tz, t_cst])
                        tf2 = P.op('dve', (lambda h, pbuf=pbuf, c=c, w=w: h.tensor_tensor(out=zC[:, c, 0:w - 1], in0=fx[:, 0:w - 1], in1=pbuf[:, 15:15 + w - 1], op=ALU.subtract)), [tf1])
                        ztk.append(tf2)
                    th2 = P.op('dve', (lambda h, pbuf=pbuf, c=c: h.tensor_copy(out=phist[:, c, :], in_=pbuf[:, Tpp:Tpp + 15])), ztk)
                    ztk.append(th2)
                if has_sample:
                    srcv = pss[:, c, :, :]
                    sh = 1
                    tprev = ptoks
                    lvl = 0
                    while sh < w:
                        dstv = pst[:, lvl % 2, :, :]
                        tprev = [P.op('dve', (lambda h, srcv=srcv, dstv=dstv, sh=sh: h.tensor_tensor(
                            out=dstv[:, :, sh:23], in0=srcv[:, :, sh:23], in1=srcv[:, :, 0:23 - sh], op=ALU.add)), tprev + ztk[-1:])]
                        srcv = dstv
                        sh *= 2
                        lvl += 1
                    tz = P.op('dve', (lambda h, srcv=srcv, c=c, w=w: h.scalar_tensor_tensor(
                        out=zC[:, c, Tpp:Tpp + 128].rearrange("p (s t) -> p s t", t=8), in0=srcv[:, :, 15:23], scalar=1.0 / w,
                        in1=pss[:, c, :, 15:23], op0=ALU.mult, op1=ALU.subtract)), tprev)
                    ztk.append(tz)
                pb_rel[c % 3] = ztk
                zc_toks += ztk
            if MSTOP <= 4:
                return

            if has_sample:
                qtok = [av(TMP_OFF + 1024 * i, [128, 512], F32) for i in range(2)]
                ptok = [av(TMP_OFF + 2048 + 1024 * i, [128, 512], F32) for i in range(2)]
                xat = [av(TMP_OFF + 4096 + 512 * i, [128, 256], F32) for i in range(2)]
                xat_rel = [[], []]
                xi = 0
                tiles = [(Th - 256, 0), (Th - 128, 1)]
                qt_toks = [[], []]
                pt_toks = [[], []]
                for hf in range(2):
                    s1i, slot1, wt1 = wslab([(wcols(win, hf * 256, 256), 0, 256)])
                    s2i, slot2, wt2 = wslab([(wcols(win, 1024 + hf * 256, 256), 0, 256)])
                    s3i, slot3, wt3 = wslab([(wcols(win, 2560 + hf * 256, 256), 0, 256)])
                    mm = None
                    for (lt, idx) in tiles:
                        bi, bank, bd = PS.acquire()
                        for k in range(8):
                            mma = P.op('pe', (lambda h, bank=bank, k=k, lt=lt, slot1=slot1: h.matmul(bank[:, 0:256], lhsT=hpart[:, k, lt:lt + 128], rhs=slot1[:, k, :],
                                                                                      start=(k == 0), stop=(k == 7))), [wt1] + hall + bd)
                        for k in range(8):
                            mm = P.op('pe', (lambda h, bank=bank, k=k, lt=lt, slot2=slot2: h.matmul(bank[:, 256:512], lhsT=hpart[:, k, lt:lt + 128], rhs=slot2[:, k, :],
                                                                                     start=(k == 0), stop=(k == 7))), [wt2] + hall)
                        xb_ = xi % 2
                        xi += 1
                        ta = P.op('act', (lambda h, bank=bank, xb_=xb_: h.activation(out=xat[xb_][:], in_=bank[:, 0:256], func=AF.Copy)), [mma, mm] + xat_rel[xb_] + zc_toks)
                        tq = P.op('dve', (lambda h, bank=bank, xb_=xb_, idx=idx, hf=hf: h.tensor_tensor(
                            out=qtok[idx][:, hf * 256:(hf + 1) * 256], in0=bank[:, 256:512], in1=xat[xb_][:], op=ALU.mult)), [mm, ta])
                        xat_rel[xb_] = [tq]
                        PS.release(bi, [tq])
                        qt_toks[idx].append(tq)
                        bi, bank, bd = PS.acquire()
                        for k in range(8):
                            mm = P.op('pe', (lambda h, bank=bank, k=k, lt=lt, slot3=slot3: h.matmul(bank[:, 0:256], lhsT=hpart[:, k, lt:lt + 128], rhs=slot3[:, k, :],
                                                                                     start=(k == 0), stop=(k == 7))), [wt3] + hall + bd)
                        tp = P.op('act', (lambda h, bank=bank, idx=idx, hf=hf: h.activation(out=ptok[idx][:, hf * 256:(hf + 1) * 256], in_=bank[:, 0:256], func=AF.Copy)), [mm] + zc_toks)
                        PS.release(bi, [tp])
                        pt_toks[idx].append(tp)
                    RG.release(s1i, [mm]); RG.release(s2i, [mm]); RG.release(s3i, [mm])
                for j in range(2):
                    outs.append(P.dma('sp', (lambda h, j=j: h.dma_start(out=ncp_d[l, j:j + 1, :], in_=qtok[0][126 + j:127 + j, :])), 'out', qt_toks[0]))
                for j in range(15):
                    outs.append(P.dma('sp', (lambda h, j=j: h.dma_start(out=npp_d[l, j:j + 1, :], in_=ptok[0][113 + j:114 + j, :])), 'out', pt_toks[0]))
                outs.append(P.dma('sp', lambda h: h.dma_start(out=ncs_d[l, :, 0, :], in_=qtok[1][6:128:8, :]), 'out', qt_toks[1]))
                outs.append(P.dma('sp', lambda h: h.dma_start(out=ncs_d[l, :, 1, :], in_=qtok[1][7:128:8, :]), 'out', qt_toks[1]))
                bt = av(TMP_OFF + 5120, [128, 512], F32)
                bguard = P.now()
                lt_ = None
                for r in range(7):
                    lt_ = P.dma('sp', (lambda h, r=r: h.dma_start(out=bt[r * 16:(r + 1) * 16, :], in_=spool_raw_d[l, :, 8 + r, :])), 'bt', bguard)
                for r in range(7):
                    outs.append(P.dma('sp', (lambda h, r=r: h.dma_start(out=nps_d[l, :, r, :], in_=bt[r * 16:(r + 1) * 16, :])), 'out', [lt_]))
                for t8 in range(8):
                    outs.append(P.dma('sp', (lambda h, t8=t8: h.dma_start(out=nps_d[l, :, 7 + t8, :], in_=ptok[1][t8:128:8, :])), 'out', pt_toks[1]))

            if MSTOP <= 5:
                return
            sig = [[av(DN_OFF + (g * 2 + i) * 1024, [128, 512], F32) for i in range(2)] for g in range(3)]
            tt_ = [av(VN_OFF + i * 1024, [128, 512], F32) for i in range(3)]
            sig_rel = [[[], []] for _ in range(3)]
            tt_rel = []
            si_ = 0
            mg_toks = []
            zall = za_toks + zb_toks + zc_toks
            woa = woa_d[l]
            wob = wob_d[l]
            for i in range(8):
                s1i, slot1, wt1 = wslab([(wcols(win, 3072 + i * 128), 0, 128), (wcols(win, 4096 + i * 128), 128, 128)])
                s2i, slot2, wt2 = wslab([(wcols(win, 5120 + i * 128), 0, 128),
                                         (woa[:, i * 128:(i + 1) * 128].rearrange("(k p) n -> p k n", p=128), 128, 128, 0, 4),
                                         (wob[:, i * 128:(i + 1) * 128].rearrange("(k p) n -> p k n", p=128), 128, 128, 4, 8)])
                mm = None
                for (c0, n) in tts:
                    lc = c0 - pc0
                    sb_ = si_ % 2
                    si_ += 1
                    banks = []
                    for g in range(3):
                        bi, bank, bd = PS.acquire()
                        slot, off, wt = [(slot1, 0, wt1), (slot1, 128, wt1), (slot2, 0, wt2)][g]
                        for k in range(8):
                            mm = P.op('pe', (lambda h, bank=bank, slot=slot, off=off, k=k, lc=lc, n=n: h.matmul(
                                bank[:, 0:n], lhsT=slot[:, k, off:off + 128], rhs=hpart[:, k, lc:lc + n], start=(k == 0), stop=(k == 7))), [wt] + hall + bd)
                        ts = P.op('act', (lambda h, bank=bank, g=g, sb_=sb_, n=n: h.activation(out=sig[g][sb_][:, 0:n], in_=bank[:, 0:n], func=AF.Sigmoid)),
                                  [mm] + sig_rel[g][sb_] + vn_toks + zb_toks + (nvs_tok if has_sample else []))
                        PS.release(bi, [ts])
                        banks.append(ts)
                    ys = []
                    for g in range(3):
                        bi, bank, bd = PS.acquire()
                        if g < 2:
                            zz = zA if g == 0 else zB
                            for k in range(4):
                                mm = P.op('pe', (lambda h, bank=bank, zz=zz, g=g, k=k, lc=lc, n=n, slot2=slot2: h.matmul(
                                    bank[:, 0:n], lhsT=slot2[:, g * 4 + k, 128:256], rhs=zz[:, k, lc:lc + n], start=(k == 0), stop=(k == 3))), [wt2] + (za_toks if g == 0 else zb_toks) + bd)
                        else:
                            gq = i // 2
                            mm = P.op('pe', (lambda h, bank=bank, gq=gq, i=i, lc=lc, n=n: h.matmul(
                                bank[:, 0:n], lhsT=poolw[:, gq, (i % 2) * 128:(i % 2) * 128 + 128], rhs=zC[:, gq, lc:lc + n], start=True, stop=True)),
                                [lay_tok[l]] + zc_toks + bd)
                        ys.append((bi, bank, mm))
                    t1 = P.op('dve', (lambda h, bank=ys[0][1], sb_=sb_, n=n: h.tensor_tensor(out=tt_[0][:, 0:n], in0=bank[:, 0:n], in1=sig[0][sb_][:, 0:n], op=ALU.mult)),
                              [ys[0][2], banks[0], sg_last] + tt_rel)
                    PS.release(ys[0][0], [t1])
                    t2 = P.op('dve', (lambda h, bank=ys[1][1], sb_=sb_, n=n: h.tensor_tensor(out=tt_[1][:, 0:n], in0=bank[:, 0:n], in1=sig[1][sb_][:, 0:n], op=ALU.mult)),
                              [ys[1][2], banks[1], sg_last])
                    PS.release(ys[1][0], [t2])
                    t3 = P.op('dve', (lambda h, bank=ys[2][1], sb_=sb_, n=n, i=i: h.scalar_tensor_tensor(
                        out=tt_[2][:, 0:n], in0=bank[:, 0:n], scalar=vecs[:, 108 + i:109 + i], in1=sig[2][sb_][:, 0:n], op0=ALU.mult, op1=ALU.mult)),
                        [ys[2][2], banks[2], lay_tok[l], sg_last])
                    PS.release(ys[2][0], [t3])
                    t4 = P.op('dve', (lambda h, n=n: h.tensor_tensor(out=tt_[0][:, 0:n], in0=tt_[0][:, 0:n], in1=tt_[1][:, 0:n], op=ALU.add)), [t1, t2])
                    t5 = P.op('dve', (lambda h, i=i, lc=lc, n=n: h.tensor_tensor(out=merged[:, i, lc:lc + n], in0=tt_[0][:, 0:n], in1=tt_[2][:, 0:n], op=ALU.add)), [t4, t3])
                    for g in range(3):
                        sig_rel[g][sb_] = [t5]
                    tt_rel = [t5]
                    mg_toks.append(t5)
                RG.release(s1i, [mm]); RG.release(s2i, [mm])
            if MSTOP <= 6:
                return
            nxt = None
            if next_tts is not None:
                nxt = norm(1, next_tts, hpart, next_tts[0][0], DN_OFF, mod_toks, extra=mg_toks)
            wo = wo_d[l]
            xl = {}
            for ip in range(8):
                if ip % 2 == 0:
                    si, slot, wt = wslab([(wcols(wo, ip * 128, 256), 0, 256)])
                mm = None
                for (c0, n) in tts:
                    lc = c0 - pc0
                    bi, bank, bd = PS.acquire()
                    for k in range(8):
                        mm = P.op('pe', (lambda h, bank=bank, slot=slot, ip=ip, k=k, lc=lc, n=n: h.matmul(
                            bank[:, 0:n], lhsT=slot[:, k, (ip % 2) * 128:(ip % 2) * 128 + 128], rhs=merged[:, k, lc:lc + n], start=(k == 0), stop=(k == 7))),
                            [wt] + mg_toks + bd)
                    t = resid_update(bank, ip, c0, n, 1, mm, mod_toks)
                    PS.release(bi, [t])
                    xl.setdefault(c0, []).append(t)
                if ip % 2 == 1:
                    RG.release(si, [mm])
            for (c0, n) in tts:
                set_xw(c0, n, xl[c0])
            return nxt

        import os
        STAGE = int(os.environ.get("KSTAGE", "99"))
        outs = []
        fin_extra = []
        if STAGE >= 1:
            load_layer_small(0)
            mj = ModJob(0)
        for l in range(nl):
            if l == 0:
                for _ in range(6):
                    mj.step()
                mj._flush()
                mod_toks = finish_mod(l, list(mj.toks), subs=(0,))
                hook1 = mj.step
            else:
                mod_toks = finish_mod(l, mj.finish())
                hook1 = None
            P.barrier()
            htoks = norm(0, TT_ALL, hfull, 0, DN_OFF, mod_toks)
            P.floor = []
            ffn(l, w1gu_d, w1dn_d, 0, htoks, hook1)
            if l == 0:
                mod_toks = mod_toks + finish_mod(l, mj.finish(), subs=(1, 2))
            for pi, tts in enumerate(PARTS):
                if STAGE == 4 and pi >= 1:
                    break
                pre = mixer_part(l, pi, tts, mod_toks, pre if pi > 0 else None, PARTS[pi + 1] if pi + 1 < len(PARTS) else None)
            if STAGE < 6:
                break
            P.barrier()
            mix_end = P.now()
            htoks = norm(2, TT_ALL, hfull, 0, DN_OFF, mod_toks)
            P.floor = []
            hook = None
            if l + 1 < nl:
                load_layer_small(l + 1, mix_end)
                mj = ModJob(l + 1)
                mj.extra = [t for v in htoks.values() for t in v]
                hook = mj.step
            ffn(l, w2gu_d, w2dn_d, 2, htoks, hook)
        P.barrier()
        norm(0, TT_ALL, None, 0, DN_OFF, [], final=True)
        P.floor = []
        ostage = [av(H_OFF + 2048 * i, [128, 1024], F32) for i in range(2)]
        ost_rel = [[], []]
        for ti in range(17):
            b = ti % 2
            cps = []
            for a in range(2):
                bi, bank, bd = PS.acquire()
                tr = None
                for j in range(4):
                    dc = a * 4 + j
                    tr = P.op('pe', (lambda h, bank=bank, dc=dc, j=j, ti=ti: h.transpose(
                        out=bank[:, j * 128:(j + 1) * 128], in_=xT[:, dc, ti * 128:(ti + 1) * 128], identity=ident[:])), xdeps(ti * 128, 128) + bd)
                if a == 0:
                    tc = P.op('act', (lambda h, bank=bank, b=b, a=a: h.activation(out=ostage[b][:, a * 512:(a + 1) * 512], in_=bank[:], func=AF.Copy)), [tr] + ost_rel[b])
                else:
                    tc = P.op('dve', (lambda h, bank=bank, b=b, a=a: h.tensor_copy(out=ostage[b][:, a * 512:(a + 1) * 512], in_=bank[:])), [tr] + ost_rel[b])
                PS.release(bi, [tc])
                cps.append(tc)
            dst = yp_d[ti * 128:(ti + 1) * 128, :] if ti < 16 else ys_d
            to = P.dma('sp', (lambda h, b=b, dst=dst: h.dma_start(out=dst, in_=ostage[b])), f"stg{b}", cps)
            ost_rel[b] = [to]
            fin_toks = [t for t in ost_rel[0] + ost_rel[1]]
        for e in ENGS:
            P.wait(e, fin_toks + outs[-1:] + fin_extra[-1:])
        P.emit(nc, sems)
    return nc


_NC = {}


def _host_prep(inp, nl=NL):
    f = lambda a: np.ascontiguousarray(np.asarray(a, dtype=np.float32))
    x_prompt = f(inp['x_prompt']); x_sample = f(inp['x_sample'])
    state_conv = f(inp['state_conv']); state_pool = f(inp['state_pool'])
    c_prompt = f(inp['c_prompt']); c_sample = f(inp['c_sample'])
    norm_g = f(inp['norm_g']); b_ada = f(inp['b_ada']); conv_w = f(inp['conv_w']); pool_scale = f(inp['pool_scale'])
    b_s = f(inp['b_s']); w_s = f(inp['w_s'])
    vecs = np.zeros((NL, 128, 116), np.float32)
    biasp = np.zeros((NL, 128, 4, 128), np.float32)
    biass = np.zeros((NL, 128, 4, 128), np.float32)
    wsT = np.zeros((NL, 128, 8, 128), np.float32)
    ws8 = np.zeros((NL, 128, 8, 128), np.float32)
    for l in range(NL):
        vecs[l, :, 0:72] = b_ada[l].reshape(72, 128).T
        vecs[l, :, 72:96] = norm_g[l].reshape(24, 128).T
        vecs[l, :, 96:108] = conv_w[l].reshape(12, 128).T
        vecs[l, :, 108:116] = pool_scale[l].reshape(8, 128).T
        bp = b_s[l].reshape(4, 2, 128).transpose(1, 0, 2)
        biasp[l] = np.repeat(bp[:, None, :, :], 64, axis=1).reshape(128, 4, 128)
        biass[l] = np.tile(biasp[l][:, :, :8], (1, 1, 16))
        wsT[l] = w_s[l].transpose(2, 0, 1)
        sm = w_s[l][:, :8, :8].transpose(2, 0, 1)
        ws8[l] = np.tile(sm, (16, 1, 16))
    p = np.arange(128)
    masks = np.zeros((128, 2, 128), np.float32)
    masks[:, 0, :] = (p[:, None] <= p[None, :])
    masks[:, 1, :] = ((p[:, None] // 8) == (p[None, :] // 8)) & ((p[:, None] % 8) <= (p[None, :] % 8))
    invcnt = np.zeros((128, 4, 16), np.float32)
    for g, w in enumerate(WINS):
        invcnt[:, g, :] = 1.0 / np.minimum(np.arange(16) + 1, w)
    shared = dict(
        vecs=vecs, fin=f(inp['final_norm_g']).reshape(8, 128).T.copy(), lng=f(inp['ln_g']),
        biasp=biasp, biass=biass, wsT=wsT, ws8=ws8, masks=masks, invcnt=invcnt, ident=np.eye(128, dtype=np.float32),
        w_ada=f(inp['w_ada']), w1_gu=f(inp['w1_gu']), w1_dn=f(inp['w1_dn']), w2_gu=f(inp['w2_gu']), w2_dn=f(inp['w2_dn']),
        w_in=f(inp['w_in']), w_out_a=f(inp['w_out_a']), w_out_b=f(inp['w_out_b']), pool_w=f(inp['pool_w']), w_o=f(inp['w_o']),
    )
    in_maps = []
    for c in range(8):
        sc = state_conv[:, c * 16:(c + 1) * 16]
        sp = state_pool[:, c * 16:(c + 1) * 16]
        c_all = np.concatenate([c_prompt[c:c + 1], c_sample[c * 16:(c + 1) * 16]], 0)
        m = dict(shared)
        m['xp'] = np.ascontiguousarray(x_prompt[c])
        m['xs'] = np.ascontiguousarray(x_sample[c * 16:(c + 1) * 16].reshape(128, D))
        m['cT'] = np.ascontiguousarray(c_all.T.reshape(8, 128, 17).transpose(1, 0, 2))
        scT_ = np.zeros((NL, 128, 4, 16, 10), np.float32)
        scT_[..., 0:2] = sc.reshape(NL, 16, 2, 4, 128).transpose(0, 4, 3, 1, 2)
        spT_ = np.zeros((NL, 128, 4, 16, 23), np.float32)
        spT_[..., 0:15] = sp.reshape(NL, 16, 15, 4, 128).transpose(0, 4, 3, 1, 2)
        m['sconvT'] = scT_
        m['spoolT'] = spT_
        m['spool_raw'] = np.ascontiguousarray(sp)
        in_maps.append(m)
    return in_maps


def kernel(**inp):
    if 'nc' not in _NC:
        _NC['nc'] = build(NL)
    nc = _NC['nc']
    in_maps = _host_prep(inp)
    res = run_bass_kernel_spmd(nc, in_maps, core_ids=list(range(8)))
    r = res.results
    y_prompt = np.stack([r[c]['yp'] for c in range(8)], 0)
    y_sample = np.concatenate([r[c]['ys'].reshape(16, 8, D) for c in range(8)], 0)
    ncp = np.stack([r[c]['ncp'] for c in range(8)], 1)
    ncs = np.concatenate([r[c]['ncs'] for c in range(8)], 1)
    npp = np.stack([r[c]['npp'] for c in range(8)], 1)
    nps = np.concatenate([r[c]['nps'] for c in range(8)], 1)
    nvs = np.concatenate([r[c]['nvs'].reshape(NL, 16, 8, 512) for c in range(8)], 1)
    return (y_prompt.astype(np.float32), y_sample.astype(np.float32), ncp.astype(np.float32), ncs.astype(np.float32),
            npp.astype(np.float32), nps.astype(np.float32), nvs.astype(np.float32))
```
